# Optimizing a Trainium2 kernel written in Bass

```python
import jax, jax.numpy as jnp
from jax import lax
import numpy as np

D_MODEL = 1024
BATCH = 8
SEQ = 4096
DEPTH = 1

CHUNK = 64
RWKV_HEADS = 8
RWKV_HEAD_DIM = 64
RWKV_WIDTH = RWKV_HEADS * RWKV_HEAD_DIM
DECAY_LORA = 64
ICLR_LORA = 64
GATE_LORA = 128
GDN_HEADS = 4
GDN_HEAD_DIM = 128
GDN_WIDTH = GDN_HEADS * GDN_HEAD_DIM
GDN_CONV = 4
FFN_HIDDEN = 2816
FFN_CONV = 3
NORM_EPS = 1e-6
L2_EPS = 1e-6
RWKV_GN_EPS = 64e-5

RWKV_SHIFT_WIDTH = 3 * RWKV_WIDTH + DECAY_LORA + ICLR_LORA + GATE_LORA
IN_SPLITS = (RWKV_SHIFT_WIDTH, 3 * GDN_WIDTH, GDN_WIDTH, GDN_HEADS, GDN_HEADS, D_MODEL, D_MODEL)
IN_WIDTH = RWKV_SHIFT_WIDTH + 4 * GDN_WIDTH + 2 * GDN_HEADS + 2 * D_MODEL

kernel_name = 'hybrid_rwkv7_gdn_gated_merge_block'


def _split(t, sizes):
    cuts = [int(c) for c in np.cumsum(sizes)[:-1]]
    return jnp.split(t, cuts, axis=-1)


def rms_norm(t, gain, eps=NORM_EPS):
    tf = t.astype(jnp.float32)
    y = tf * lax.rsqrt(jnp.mean(tf * tf, axis=-1, keepdims=True) + eps)
    return (y * gain.astype(jnp.float32)).astype(t.dtype)


def l2norm(t):
    tf = t.astype(jnp.float32)
    return (tf * lax.rsqrt(jnp.sum(tf * tf, axis=-1, keepdims=True) + L2_EPS)).astype(t.dtype)


def causal_depthwise_conv(t, w):
    width = w.shape[0]
    T = t.shape[1]
    tp = jnp.pad(t, ((0, 0), (width - 1, 0), (0, 0)))
    out = tp[:, 0:T] * w[0]
    for i in range(1, width):
        out = out + tp[:, i:i + T] * w[i]
    return out


def token_shift(t):
    return jnp.pad(t, ((0, 0), (1, 0), (0, 0)))[:, :-1]


def group_norm_heads(y, w, b):
    yf = y.astype(jnp.float32)
    mean = jnp.mean(yf, axis=-1, keepdims=True)
    var = jnp.mean(jnp.square(yf - mean), axis=-1, keepdims=True)
    yn = (yf - mean) * lax.rsqrt(var + RWKV_GN_EPS)
    H, D = y.shape[-2], y.shape[-1]
    return (yn * w.reshape(H, D) + b.reshape(H, D)).astype(y.dtype)


def wkv7_scan(r, w, k, v, a, b):
    dtype = r.dtype
    B, T, H, D = r.shape
    xs = tuple(jnp.moveaxis(t.astype(jnp.float32), 1, 0) for t in (r, w, k, v, a, b))

    def step(S, inp):
        r_t, w_t, k_t, v_t, a_t, b_t = inp
        sa = jnp.einsum('bhvk,bhk->bhv', S, a_t)
        S = S * w_t[:, :, None, :] + sa[..., None] * b_t[:, :, None, :] + v_t[..., None] * k_t[:, :, None, :]
        y = jnp.einsum('bhvk,bhk->bhv', S, r_t)
        return S, y

    S0 = jnp.zeros((B, H, D, D), jnp.float32)
    _, y = lax.scan(step, S0, xs)
    return jnp.moveaxis(y, 0, 1).astype(dtype)


def rwkv7_mix(p, mu, w0, w2, a0, a2, g2, k_k, k_a, r_k, ln_w, ln_b):
    B, T, _ = p.shape
    p = p + (token_shift(p) - p) * mu
    r, k, v, wl, al, gl = _split(p, (RWKV_WIDTH, RWKV_WIDTH, RWKV_WIDTH, DECAY_LORA, ICLR_LORA, GATE_LORA))
    w_log = -jax.nn.softplus(-(w0 + jnp.tanh(wl) @ w2)) - 0.5
    a = jax.nn.sigmoid(a0 + al @ a2)
    g = jax.nn.sigmoid(gl) @ g2

    def heads(t):
        return t.reshape(B, T, RWKV_HEADS, RWKV_HEAD_DIM)

    kk = l2norm(heads(k * k_k))
    k = k * (1 + (a - 1) * k_a)
    r_h, k_h, v_h, a_h = heads(r), heads(k), heads(v), heads(a)
    decay = jnp.exp(-jnp.exp(heads(w_log).astype(jnp.float32)))
    y = wkv7_scan(r_h, decay, k_h, v_h, -kk, kk * a_h)
    y = group_norm_heads(y, ln_w, ln_b)
    y = y + jnp.sum(r_h * k_h * r_k, axis=-1, keepdims=True) * v_h
    return y.reshape(B, T, RWKV_WIDTH) * g


def chunk_gated_delta_rule(q, k, v, g, beta):
    dtype = v.dtype
    B, T, H, Dk = q.shape
    Dv = v.shape[-1]
    N = T // CHUNK

    def to_chunks(t):
        t = t.astype(jnp.float32).reshape((B, N, CHUNK, H) + t.shape[3:])
        return jnp.moveaxis(t, 3, 1)

    q = to_chunks(q) * (Dk ** -0.5)
    k, v, g, beta = to_chunks(k), to_chunks(v), to_chunks(g), to_chunks(beta)
    gc = jnp.cumsum(g, axis=-1)
    causal = jnp.tril(jnp.ones((CHUNK, CHUNK), bool))
    strict = jnp.tril(jnp.ones((CHUNK, CHUNK), bool), -1)
    diff = gc[..., :, None] - gc[..., None, :]
    decay = jnp.where(causal, jnp.exp(jnp.where(causal, diff, 0.0)), 0.0)
    k_beta = k * beta[..., None]
    v_beta = v * beta[..., None]
    Lmat = jnp.where(strict, jnp.einsum('bhncd,bhnsd->bhncs', k_beta, k) * decay, 0.0)
    eye = jnp.eye(CHUNK, dtype=jnp.float32)
    Tinv = lax.linalg.triangular_solve(Lmat + eye, jnp.broadcast_to(eye, Lmat.shape),
                                       left_side=True, lower=True, unit_diagonal=True)
    u = jnp.einsum('bhncs,bhnsd->bhncd', Tinv, v_beta)
    wk = jnp.einsum('bhncs,bhnsd->bhncd', Tinv, k_beta * jnp.exp(gc)[..., None])
    attn = jnp.where(causal, jnp.einsum('bhncd,bhnsd->bhncs', q, k) * decay, 0.0)
    q_dec = q * jnp.exp(gc)[..., None]
    g_last = gc[..., -1]
    k_dec = k * jnp.exp(g_last[..., None] - gc)[..., None]
    xs = (jnp.moveaxis(q_dec, 2, 0), jnp.moveaxis(wk, 2, 0), jnp.moveaxis(u, 2, 0),
          jnp.moveaxis(attn, 2, 0), jnp.moveaxis(k_dec, 2, 0), jnp.moveaxis(g_last, 2, 0))

    def step(S, inp):
        q_n, w_n, u_n, attn_n, k_n, gl_n = inp
        v_new = u_n - jnp.einsum('bhcd,bhde->bhce', w_n, S)
        o = jnp.einsum('bhcd,bhde->bhce', q_n, S) + jnp.einsum('bhcs,bhse->bhce', attn_n, v_new)
        S = S * jnp.exp(gl_n)[..., None, None] + jnp.einsum('bhcd,bhce->bhde', k_n, v_new)
        return S, o

    S0 = jnp.zeros((B, H, Dk, Dv), jnp.float32)
    _, o = lax.scan(step, S0, xs)
    o = jnp.transpose(o, (1, 0, 3, 2, 4)).reshape(B, T, H, Dv)
    return o.astype(dtype)


def gated_deltanet_mix(qkv, z, a_raw, b_raw, conv_w, a_log, dt_bias, norm_w):
    B, T, _ = qkv.shape
    qkv = jax.nn.silu(causal_depthwise_conv(qkv, conv_w))
    q, k, v = _split(qkv, (GDN_WIDTH, GDN_WIDTH, GDN_WIDTH))
    q = l2norm(q.reshape(B, T, GDN_HEADS, GDN_HEAD_DIM))
    k = l2norm(k.reshape(B, T, GDN_HEADS, GDN_HEAD_DIM))
    v = v.reshape(B, T, GDN_HEADS, GDN_HEAD_DIM)
    beta = jax.nn.sigmoid(b_raw)
    g = -jnp.exp(a_log.astype(jnp.float32)) * jax.nn.softplus(a_raw.astype(jnp.float32) + dt_bias.astype(jnp.float32))
    o = chunk_gated_delta_rule(q, k, v, g, beta)
    o = rms_norm(o, norm_w) * jax.nn.silu(z.reshape(B, T, GDN_HEADS, GDN_HEAD_DIM))
    return o.reshape(B, T, GDN_WIDTH)


def setup_inputs(seed: int = 0) -> dict:
    key = jax.random.key(seed)
    ks = jax.random.split(key, 32)
    L = DEPTH

    def nrm(k, shape, scale):
        return jax.random.normal(k, shape, jnp.float32) * scale

    dt = jnp.exp(jax.random.uniform(ks[17], (L, GDN_HEADS), minval=float(np.log(1e-3)), maxval=float(np.log(1e-1))))
    return {
        'x': nrm(ks[0], (BATCH, SEQ, D_MODEL), 1.0),
        'norm1_g': 1.0 + nrm(ks[1], (L, D_MODEL), 0.02),
        'w_in': nrm(ks[2], (L, D_MODEL, IN_WIDTH), D_MODEL ** -0.5),
        'rwkv_mu': jax.random.uniform(ks[3], (L, RWKV_SHIFT_WIDTH)),
        'rwkv_w0': jax.random.uniform(ks[4], (L, RWKV_WIDTH), minval=-6.5, maxval=-1.0),
        'rwkv_w2': nrm(ks[5], (L, DECAY_LORA, RWKV_WIDTH), 0.5 * DECAY_LORA ** -0.5),
        'rwkv_a0': nrm(ks[6], (L, RWKV_WIDTH), 0.1),
        'rwkv_a2': nrm(ks[7], (L, ICLR_LORA, RWKV_WIDTH), 0.5 * ICLR_LORA ** -0.5),
        'rwkv_g2': nrm(ks[8], (L, GATE_LORA, RWKV_WIDTH), GATE_LORA ** -0.5),
        'rwkv_k_k': 0.85 + nrm(ks[9], (L, RWKV_WIDTH), 0.05),
        'rwkv_k_a': 1.0 + nrm(ks[10], (L, RWKV_WIDTH), 0.05),
        'rwkv_r_k': nrm(ks[11], (L, RWKV_HEADS, RWKV_HEAD_DIM), 0.1),
        'rwkv_ln_w': 1.0 + nrm(ks[12], (L, RWKV_WIDTH), 0.02),
        'rwkv_ln_b': nrm(ks[13], (L, RWKV_WIDTH), 0.02),
        'rwkv_proj': nrm(ks[14], (L, RWKV_WIDTH, D_MODEL), RWKV_WIDTH ** -0.5),
        'gdn_conv_w': nrm(ks[15], (L, GDN_CONV, 3 * GDN_WIDTH), GDN_CONV ** -0.5),
        'gdn_a_log': jnp.log(jax.random.uniform(ks[16], (L, GDN_HEADS), minval=1.0, maxval=16.0)),
        'gdn_dt_bias': dt + jnp.log(-jnp.expm1(-dt)),
        'gdn_norm_w': 1.0 + nrm(ks[18], (L, GDN_HEAD_DIM), 0.02),
        'gdn_proj': nrm(ks[19], (L, GDN_WIDTH, D_MODEL), GDN_WIDTH ** -0.5),
        'w_out': nrm(ks[20], (L, D_MODEL, D_MODEL), D_MODEL ** -0.5),
        'norm2_g': 1.0 + nrm(ks[21], (L, D_MODEL), 0.02),
        'ffn_up': nrm(ks[22], (L, D_MODEL, 2 * FFN_HIDDEN), D_MODEL ** -0.5),
        'ffn_conv_w': nrm(ks[23], (L, FFN_CONV, 2 * FFN_HIDDEN), FFN_CONV ** -0.5),
        'ffn_down': nrm(ks[24], (L, FFN_HIDDEN, D_MODEL), FFN_HIDDEN ** -0.5),
        'final_g': 1.0 + nrm(ks[25], (D_MODEL,), 0.02),
    }


def reference(x, norm1_g, w_in, rwkv_mu, rwkv_w0, rwkv_w2, rwkv_a0, rwkv_a2, rwkv_g2, rwkv_k_k,
              rwkv_k_a, rwkv_r_k, rwkv_ln_w, rwkv_ln_b, rwkv_proj, gdn_conv_w, gdn_a_log, gdn_dt_bias,
              gdn_norm_w, gdn_proj, w_out, norm2_g, ffn_up, ffn_conv_w, ffn_down, final_g):
    for l in range(DEPTH):
        u = rms_norm(x, norm1_g[l])
        p = u @ w_in[l]
        p_rwkv, qkv, z, a_raw, b_raw, gate_a, gate_b = _split(p, IN_SPLITS)
        y_a = rwkv7_mix(p_rwkv, rwkv_mu[l], rwkv_w0[l], rwkv_w2[l], rwkv_a0[l], rwkv_a2[l], rwkv_g2[l],
                        rwkv_k_k[l], rwkv_k_a[l], rwkv_r_k[l], rwkv_ln_w[l], rwkv_ln_b[l]) @ rwkv_proj[l]
        y_b = gated_deltanet_mix(qkv, z, a_raw, b_raw, gdn_conv_w[l], gdn_a_log[l], gdn_dt_bias[l],
                                 gdn_norm_w[l]) @ gdn_proj[l]
        mixed = jax.nn.sigmoid(gate_a) * y_a + jax.nn.sigmoid(gate_b) * y_b
        x = x + mixed @ w_out[l]
        h = rms_norm(x, norm2_g[l]) @ ffn_up[l]
        h = causal_depthwise_conv(h, ffn_conv_w[l])
        h_gate, h_up = _split(h, (FFN_HIDDEN, FFN_HIDDEN))
        x = x + (jax.nn.silu(h_gate) * h_up) @ ffn_down[l]
    return rms_norm(x, final_g)
```

```python
from contextlib import ExitStack
import numpy as np
import concourse.bass as bass
import concourse.mybir as mybir
from concourse.bass_utils import run_bass_kernel_spmd

F32 = mybir.dt.float32
BF16 = mybir.dt.bfloat16
AF = mybir.ActivationFunctionType
ALU = mybir.AluOpType

D = 1024
INW = 5896
FH = 2816
BIG = 32768.0
SEM_LIMIT = 30000


class Buf:
    __slots__ = ("name", "w", "r", "dsem", "dcnt")

    def __init__(self, name):
        self.name = name
        self.w = None
        self.r = {}
        self.dsem = None
        self.dcnt = 0


class View:
    __slots__ = ("ap", "bufs")

    def __init__(self, ap, bufs):
        self.ap = ap
        self.bufs = bufs


class Tile:
    def __init__(self, sched, name, shape, dtype, space="sbuf", split=False):
        nc = sched.nc
        if space == "sbuf":
            self.t = sched.stack.enter_context(nc.sbuf_tensor("t_" + name, list(shape), dtype))
        else:
            self.t = sched.stack.enter_context(nc.psum_tensor("t_" + name, list(shape), dtype))
        self.name = name
        self.split = split
        if split:
            self.bufs = [Buf(f"{name}.{i}") for i in range(shape[1])]
        else:
            self.bufs = [Buf(name)]

    def __getitem__(self, key):
        return View(self.t[key], self.bufs)

    def sub(self, i, key=None):
        ap = self.t[:, i] if key is None else self.t[key]
        return View(ap, [self.bufs[i]] if self.split else self.bufs)


class DramT:
    def __init__(self, ap, name):
        self.ap = ap
        self.buf = Buf(name)

    def __getitem__(self, key):
        return View(self.ap[key], [self.buf])


class EngState:
    def __init__(self, name, eng):
        self.name = name
        self.eng = eng
        self.sem = None
        self.cnt = 0
        self.pending = False
        self.seen = {}


class Sched:
    def __init__(self, nc, stack):
        self.nc = nc
        self.stack = stack
        self.semstack = stack
        self.nsem = 0
        self.E = {}
        for n, e in (("pe", nc.tensor), ("act", nc.scalar), ("dve", nc.vector), ("pool", nc.gpsimd), ("sp", nc.sync)):
            es = EngState(n, e)
            es.sem = self.new_sem(n)
            self.E[n] = es
        self.all_dsems = []
        self.ninst = 0
        self.dead = False

    def new_sem(self, name):
        self.nsem += 1
        return self.semstack.enter_context(self.nc.semaphore(f"{name}_{self.nsem}"))

    def _need(self, reads, writes):
        need = {}

        def add(ev):
            if ev is None:
                return
            s, v = ev
            k = id(s)
            if k not in need or need[k][1] < v:
                need[k] = (s, v)

        for b in reads:
            add(b.w)
        for b in writes:
            add(b.w)
            for k, (s, v) in b.r.items():
                add((s, v))
        return need

    def _wait(self, es, need, skip_self=True):
        for k, (s, v) in need.items():
            if skip_self and s is es.sem:
                continue
            if es.seen.get(k, 0) >= v:
                continue
            es.eng.wait_ge(s, v)
            es.seen[k] = v

    @staticmethod
    def _bufs(views):
        out = []
        for v in views:
            if isinstance(v, View):
                out.extend(v.bufs)
        return out

    def op(self, en, fn, outs, ins, inc=True, order_self=False):
        if self.dead:
            return
        es = self.E[en]
        R = self._bufs(ins)
        W = self._bufs(outs)
        need = self._need(R, W)
        self._wait(es, need, skip_self=(en == "pe"))
        ins_ = fn(es.eng)
        self.ninst += 1
        if es.cnt >= SEM_LIMIT and inc and not es.pending:
            es.sem = self.new_sem(es.name)
            es.cnt = 0
        if inc:
            es.cnt += 1
            ins_.then_inc(es.sem, 1)
            ev = (es.sem, es.cnt)
            es.pending = False
        else:
            ev = (es.sem, es.cnt + 1)
            es.pending = True
        k = id(ev[0])
        for b in R:
            if k not in b.r or b.r[k][1] < ev[1]:
                b.r[k] = ev
        for b in W:
            b.w = ev
            b.r = {}

    def dma(self, qn, out, in_, sembuf=None):
        if self.dead:
            return
        es = self.E[qn]
        R = self._bufs([in_])
        W = self._bufs([out])
        need = self._need(R, W)
        self._wait(es, need, skip_self=False)
        sb = sembuf if sembuf is not None else (W[0] if W else R[0])
        if sb.dsem is None:
            sb.dsem = self.new_sem("d_" + sb.name.replace(".", "_"))
            self.all_dsems.append(sb)
        es.eng.dma_start(out=out.ap, in_=in_.ap).then_inc(sb.dsem, 16)
        self.ninst += 1
        sb.dcnt += 16
        ev = (sb.dsem, sb.dcnt)
        k = id(ev[0])
        for b in R:
            if k not in b.r or b.r[k][1] < ev[1]:
                b.r[k] = ev
        for b in W:
            b.w = ev
            b.r = {}

    def barrier(self):
        if self.dead:
            return
        for es in self.E.values():
            for fs in self.E.values():
                if fs is es or fs.cnt == 0:
                    continue
                if es.seen.get(id(fs.sem), 0) < fs.cnt:
                    es.eng.wait_ge(fs.sem, fs.cnt)
                    es.seen[id(fs.sem)] = fs.cnt
            for sb in self.all_dsems:
                if es.seen.get(id(sb.dsem), 0) < sb.dcnt:
                    es.eng.wait_ge(sb.dsem, sb.dcnt)
                    es.seen[id(sb.dsem)] = sb.dcnt


def A(v):
    return v.ap if isinstance(v, View) else v


PV_FIELDS = [("g1", 8), ("mu", 14), ("w0", 4), ("a0", 4), ("kk", 4), ("ka", 4), ("rk", 4), ("lnw", 4),
             ("lnb", 4), ("cw", 48), ("nw", 1), ("g2n", 8), ("fcw", 132), ("alog", 4), ("dtb", 4)]
PV_OFF = {}
_o = 0
for _n, _w in PV_FIELDS:
    PV_OFF[_n] = (_o, _w)
    _o += _w
PV_W = _o

CONST_NAMES = ["ident", "ones", "su", "iu", "su2", "iu2", "sl", "mblo", "mbup", "blk", "blkm"]
CI = {n: i for i, n in enumerate(CONST_NAMES)}
NCONST = len(CONST_NAMES)


def make_consts():
    i = np.arange(128)
    s = i[:, None]
    t = i[None, :]
    ident = (s == t).astype(np.float32)
    ones = np.ones((128, 128), np.float32)
    su = (s < t).astype(np.float32)
    iu = (s <= t).astype(np.float32)
    sl = (t < s).astype(np.float32)
    mblo = np.where(t >= s, BIG, 0.0).astype(np.float32)
    mbup = np.where(t < s, -BIG, 0.0).astype(np.float32)
    blk = ((s // 64) == (t // 64)).astype(np.float32)
    blkm = blk / 64.0
    return np.concatenate([ident, ones, su, iu, su, iu, sl, mblo, mbup, blk, blkm], axis=1)


def chunked(v, n):
    return np.ascontiguousarray(np.asarray(v, np.float32).reshape(n, 128).T)


def make_pv(inp):
    pv = np.zeros((128, PV_W), np.float32)

    def put(name, arr):
        o, w = PV_OFF[name]
        assert arr.shape == (128, w), (name, arr.shape)
        pv[:, o:o + w] = arr

    put("g1", chunked(inp["norm1_g"][0], 8))
    put("mu", chunked(inp["rwkv_mu"][0], 14))
    put("w0", chunked(inp["rwkv_w0"][0], 4))
    put("a0", chunked(inp["rwkv_a0"][0], 4))
    put("kk", chunked(inp["rwkv_k_k"][0], 4))
    put("ka", chunked(inp["rwkv_k_a"][0], 4))
    put("rk", chunked(inp["rwkv_r_k"][0].reshape(-1), 4))
    put("lnw", chunked(inp["rwkv_ln_w"][0], 4))
    put("lnb", chunked(inp["rwkv_ln_b"][0], 4))
    cw = np.asarray(inp["gdn_conv_w"][0], np.float32)
    put("cw", np.ascontiguousarray(cw.reshape(4, 12, 128).transpose(2, 1, 0).reshape(128, 48)))
    put("nw", np.asarray(inp["gdn_norm_w"][0], np.float32).reshape(128, 1))
    put("g2n", chunked(inp["norm2_g"][0], 8))
    fcw = np.asarray(inp["ffn_conv_w"][0], np.float32)
    put("fcw", np.ascontiguousarray(fcw.reshape(3, 44, 128).transpose(2, 1, 0).reshape(128, 132)))
    put("alog", np.broadcast_to(np.asarray(inp["gdn_a_log"][0], np.float32)[None, :], (128, 4)))
    put("dtb", np.broadcast_to(np.asarray(inp["gdn_dt_bias"][0], np.float32)[None, :], (128, 4)))
    return pv


class StopBuild(Exception):
    pass


def build(T, stop=None):
    NT = T // 128

    def chk(name):
        if stop == name:
            chk.S.dead = True
    nc = bass.Bass("TRN2", target_bir_lowering=False)

    def dram_in(name, shape):
        return DramT(nc.dram_tensor(name, list(shape), F32, kind="ExternalInput").ap(), name)

    x_d = dram_in("x", [T, D])
    win_d = dram_in("w_in", [D, INW])
    w2_d = dram_in("w2", [64, 512])
    a2_d = dram_in("a2", [64, 512])
    g2_d = dram_in("g2", [128, 512])
    rproj_d = dram_in("rproj", [512, D])
    gproj_d = dram_in("gproj", [512, D])
    wout_d = dram_in("wout", [D, D])
    up_d = dram_in("ffn_up", [D, 2 * FH])
    down_d = dram_in("ffn_down", [FH, D])
    fg_d = dram_in("final_g", [D])
    pv_d = dram_in("pv", [128, PV_W])
    c_d = dram_in("consts", [128, NCONST * 128])
    y_d = DramT(nc.dram_tensor("y", [T, D], F32, kind="ExternalOutput").ap(), "y")
    x1_d = DramT(nc.dram_tensor("x1s", [T, D], F32, kind="Internal").ap(), "x1s")

    with ExitStack() as gstack:
        S = Sched(nc, gstack)
        chk.S = S

        def PE(fn, outs, ins, inc=True):
            S.op("pe", fn, outs, ins, inc=inc)

        def mm(out, lhsT, rhs, start=True, stop=True):
            S.op("pe", lambda e: e.matmul(A(out), lhsT=A(lhsT), rhs=A(rhs), start=start, stop=stop),
                 [out], [lhsT, rhs], inc=stop)

        def tr(out, in_, ident):
            S.op("pe", lambda e: e.transpose(A(out), A(in_), A(ident)), [out], [in_, ident])

        def act(out, in_, func, bias=None, scale=None, accum=None):
            kw = {}
            ins = [in_]
            if bias is not None:
                kw["bias"] = A(bias)
                ins.append(bias)
            if scale is not None:
                kw["scale"] = A(scale)
                ins.append(scale)
            outs = [out]
            if accum is not None:
                kw["accum_out"] = A(accum)
                outs.append(accum)
            S.op("act", lambda e: e.activation(out=A(out), in_=A(in_), func=func, **kw), outs, ins)

        def tt(out, a, b, op, eng="dve"):
            S.op(eng, lambda e: e.tensor_tensor(out=A(out), in0=A(a), in1=A(b), op=op), [out], [a, b],
                 order_self=(eng == "pool"))

        def ts(out, a, s1, op0, s2=None, op1=None, eng="dve"):
            if op1 is None:
                S.op(eng, lambda e: e.tensor_scalar(out=A(out), in0=A(a), scalar1=A(s1), scalar2=None, op0=op0),
                     [out], [a, s1], order_self=(eng == "pool"))
            else:
                S.op(eng, lambda e: e.tensor_scalar(out=A(out), in0=A(a), scalar1=A(s1), scalar2=A(s2), op0=op0, op1=op1),
                     [out], [a, s1, s2], order_self=(eng == "pool"))

        def stt(out, a, s, b, op0, op1, eng="dve"):
            S.op(eng, lambda e: e.scalar_tensor_tensor(out=A(out), in0=A(a), scalar=A(s), in1=A(b), op0=op0, op1=op1),
                 [out], [a, s, b], order_self=(eng == "pool"))

        def cp(out, in_, eng="dve"):
            if eng == "act":
                act(out, in_, AF.Copy)
            else:
                S.op(eng, lambda e: e.tensor_copy(out=A(out), in_=A(in_)), [out], [in_], order_self=(eng == "pool"))

        def memset(out, val, eng="dve"):
            S.op(eng, lambda e: e.memset(A(out), val), [out], [], order_self=(eng == "pool"))

        def recip(out, in_):
            S.op("dve", lambda e: e.reciprocal(out=A(out), in_=A(in_)), [out], [in_])

        def scan_cumsum(out, ones, data):
            S.op("dve", lambda e: e.tensor_tensor_scan(out=A(out), data0=A(ones), data1=A(data), initial=0.0,
                                                       op0=ALU.mult, op1=ALU.add), [out], [ones, data])

        c32 = Tile(S, "c32", [128, NCONST * 128], F32)
        pv = Tile(S, "pv", [128, PV_W], F32)
        identb = Tile(S, "identb", [128, 128], BF16)
        dv = Tile(S, "dv", [128, 16], F32)
        psf = [Tile(S, f"psf{i}", [128, 512], F32, space="psum") for i in range(6)]
        psb = [Tile(S, f"psb{i}", [128, 1024], BF16, space="psum") for i in range(2)]
        rr = {"f": 0, "b": 0}

        def PF():
            t = psf[rr["f"] % 6]
            rr["f"] += 1
            return t

        def PB():
            t = psb[rr["b"] % 2]
            rr["b"] += 1
            return t

        def C(name, w=1):
            i = CI[name]
            return c32[:, i * 128:(i + w) * 128]

        def pvv(name, j=None, w=1):
            o, wd = PV_OFF[name]
            if j is None:
                return pv[:, o:o + wd]
            return pv[:, o + j:o + j + w]

        S.dma("sp", c32[:, :], c_d[:, :])
        S.dma("sp", pv[:, :], pv_d[:, :])

        try:
            cp(identb[:, :], C("ident"))
            ts(dv[:, 0:4], pvv("ka"), -1.0, ALU.mult, 1.0, ALU.add)
            act(dv[:, 4:8], pvv("alog"), AF.Exp)
            ts(dv[:, 4:8], dv[:, 4:8], -1.0, ALU.mult)

            with ExitStack() as st1:
                S.stack = st1
                winb = Tile(S, "winb", [128, 8, INW], BF16)
                rprojb = Tile(S, "rprojb", [128, 4, D], BF16)
                gprojb = Tile(S, "gprojb", [128, 4, D], BF16)
                woutb = Tile(S, "woutb", [128, 8, D], BF16)
                lorab = Tile(S, "lorab", [128, 512], BF16)
                g2b = Tile(S, "g2b", [128, 512], BF16)

                with ExitStack() as stg_stack:
                    S.stack = stg_stack
                    stg = [Tile(S, f"stg{i}", [128, 1536], F32) for i in range(2)]
                    k = [0]

                    def load_cast(dst, src, ncols, scale=None, prow=slice(0, 128)):
                        s = stg[k[0] % 2]
                        S.dma("sp", s[prow, 0:ncols], src)
                        if scale is not None:
                            if k[0] % 2 == 0:
                                act(dst, s[prow, 0:ncols], AF.Copy, scale=scale)
                            else:
                                ts(dst, s[prow, 0:ncols], scale, ALU.mult)
                        else:
                            cp(dst, s[prow, 0:ncols], eng="act" if k[0] % 2 == 0 else "dve")
                        k[0] += 1

                    CB = 1474
                    for kc in range(8):
                        for cb in range(4):
                            load_cast(winb[:, kc, cb * CB:(cb + 1) * CB], win_d[kc * 128:(kc + 1) * 128, cb * CB:(cb + 1) * CB],
                                      CB, scale=pvv("g1", kc))
                    for kc in range(4):
                        load_cast(rprojb[:, kc, :], rproj_d[kc * 128:(kc + 1) * 128, :], D)
                        load_cast(gprojb[:, kc, :], gproj_d[kc * 128:(kc + 1) * 128, :], D)
                    for kc in range(8):
                        load_cast(woutb[:, kc, :], wout_d[kc * 128:(kc + 1) * 128, :], D)
                    load_cast(lorab[0:64, :], w2_d[:, :], 512, prow=slice(0, 64))
                    load_cast(lorab[64:128, :], a2_d[:, :], 512, prow=slice(64, 128))
                    load_cast(g2b[:, :], g2_d[:, :], 512)
                    S.barrier()
                S.stack = st1

                chk("setup1")
                xt = Tile(S, "xt", [128, D], F32)
                sc = Tile(S, "sc", [128, 16], F32)
                junk = Tile(S, "junk", [128, D], BF16)
                ub = Tile(S, "ub", [128, D], BF16)
                uT = Tile(S, "uT", [128, 8, 128], BF16)
                halo_r = Tile(S, "halo_r", [128, 14], F32)
                praw = [Tile(S, f"praw{i}", [128, 132], F32) for i in range(2)]
                pl = Tile(S, "pl", [128, 14, 128], F32, split=True)
                tw_al = Tile(S, "tw_al", [128, 128], BF16)
                sgl = Tile(S, "sgl", [128, 128], BF16)
                gT = Tile(S, "gT", [128, 4, 128], F32, split=True)
                bon = Tile(S, "bon", [128, 4, 128], F32, split=True)
                NF = 14
                tf = [Tile(S, f"tf{i}", [128, 128], F32) for i in range(NF)]
                ar = Tile(S, "ar", [128, 4, 256], BF16, split=True)
                btl = Tile(S, "btl", [128, 4, 128], BF16, split=True)
                ktl = Tile(S, "ktl", [128, 4, 128], BF16, split=True)
                tb = [Tile(S, f"tb{i}", [128, 128], BF16) for i in range(6)]
                tokm = Tile(S, "tokm", [128, 4, 512], BF16, split=True)
                vpad = [Tile(S, f"vpad{h}", [128, 4, 128], BF16, split=True) for h in range(2)]
                upad = [Tile(S, f"upad{h}", [128, 128], BF16) for h in range(2)]
                AM = [Tile(S, f"AM{h}", [128, 512], BF16) for h in range(2)]
                NTP = [Tile(S, f"NTP{i}", [128, 384], F32) for i in range(4)]
                tinvT = [Tile(S, f"tinvT{h}", [128, 128], BF16) for h in range(2)]
                Xb = Tile(S, "Xb", [128, 128], BF16)
                yfin = Tile(S, "yfin", [128, 4, 128], BF16, split=True)
                H32 = Tile(S, "H32", [128, 4, 128], F32, split=True)
                Hbd = Tile(S, "Hbd", [128, 4, 128], BF16, split=True)
                S32 = Tile(S, "S32", [128, 4, 128], F32, split=True)
                Sb = Tile(S, "Sb", [128, 4, 128], BF16, split=True)
                halo_g = Tile(S, "halo_g", [128, 12, 3], F32)
                qT = Tile(S, "qT", [128, 4, 128], BF16, split=True)
                kT = Tile(S, "kT", [128, 4, 128], BF16, split=True)
                vTb = Tile(S, "vTb", [128, 4, 128], BF16, split=True)
                siluz = Tile(S, "siluz", [128, 4, 128], F32, split=True)
                gt = Tile(S, "gt", [128, 64], F32)
                ybT = Tile(S, "ybT", [128, 4, 128], BF16, split=True)
                mixT = Tile(S, "mixT", [128, 8, 128], BF16, split=True)
                x1 = Tile(S, "x1", [128, D], F32)

                memset(halo_r[:, :], 0.0)
                memset(halo_g[:, :, :], 0.0)
                memset(H32[:, :, :], 0.0)
                memset(Hbd[:, :, :], 0.0)
                memset(S32[:, :, :], 0.0)
                memset(Sb[:, :, :], 0.0)
                for h in range(2):
                    memset(vpad[h][:, :, :], 0.0)
                    memset(upad[h][:, :], 0.0)

                def proj_chunk(ps_view, col0, ncols=128):
                    for kc in range(8):
                        mm(ps_view, winb[:, kc, col0:col0 + ncols], uT[:, kc, :], start=(kc == 0), stop=(kc == 7))

                def neumann_batch(systems):
                    idb = C("ident")
                    cur = [None] * len(systems)
                    for lvl in range(7):
                        for si, (N0, T0, outT, bA, bB) in enumerate(systems):
                            ps = PF()
                            if lvl == 0:
                                Nk, Tk, Pk = N0, T0, None
                            else:
                                c_ = cur[si]
                                Nk, Tk, Pk = c_[:, 0:128], c_[:, 128:256], c_[:, 256:384]
                            if lvl <= 5:
                                mm(ps[:, 0:128], Tk, Nk)
                            if lvl <= 4:
                                mm(ps[:, 128:256], Nk, Tk)
                            if lvl == 0:
                                mm(ps[:, 256:384], idb, T0, start=True, stop=False)
                                mm(ps[:, 256:384], idb, idb, start=False, stop=True)
                            else:
                                mm(ps[:, 256:384], Nk, Pk, start=True, stop=False)
                                mm(ps[:, 256:384], idb, Pk, start=False, stop=True)
                            e1 = "act" if si % 2 == 0 else "dve"
                            e2 = "dve" if si % 2 == 0 else "act"
                            if lvl == 6:
                                cp(outT, ps[:, 256:384], eng=e1)
                            else:
                                nxt = bB if lvl % 2 == 0 else bA
                                if lvl == 5:
                                    cp(nxt[:, 0:128], ps[:, 0:128], eng=e1)
                                    cp(nxt[:, 256:384], ps[:, 256:384], eng=e2)
                                else:
                                    cp(nxt[:, 0:384], ps[:, 0:384], eng=e1)
                                cur[si] = nxt

                for it in range(NT):
                    S.dma("sp", xt[:, :], x_d[it * 128:(it + 1) * 128, :])
                    memset(sc[:, 0:1], 0.0)
                    act(junk[:, :], xt[:, :], AF.Square, accum=sc[:, 0:1])
                    act(sc[:, 1:2], sc[:, 0:1], AF.Sqrt, bias=1e-6, scale=1.0 / D)
                    recip(sc[:, 2:3], sc[:, 1:2])
                    ts(ub[:, :], xt[:, :], sc[:, 2:3], ALU.mult)
                    pb = PB()
                    for kc in range(8):
                        tr(pb[:, kc * 128:(kc + 1) * 128], ub[:, kc * 128:(kc + 1) * 128], identb[:, :])
                    S.op("act", lambda e: e.activation(out=uT.t[:, :, :], in_=pb.t[:, 0:1024].rearrange("p (a b) -> p a b", a=8), func=AF.Copy),
                         [uT[:, :, :]], [pb[:, :]])

                    chk("norm")
                    for c in [12, 13] + list(range(12)):
                        ps = PF()
                        proj_chunk(ps[:, 0:128], c * 128)
                        pr = praw[c % 2]
                        cp(pr[:, 1:129], ps[:, 0:128], eng="act")
                        cp(pr[:, 0:1], halo_r[:, c:c + 1], eng="dve")
                        cp(halo_r[:, c:c + 1], pr[:, 128:129], eng="dve")
                        d_ = tf[0]
                        tt(d_[:, :], pr[:, 0:128], pr[:, 1:129], ALU.subtract)
                        stt(pl.sub(c), d_[:, :], pvv("mu", c), pr[:, 1:129], ALU.mult, ALU.add)
                        if c == 12:
                            act(tw_al[0:64, :], pl.sub(12, (slice(0, 64), 12)), AF.Tanh)
                            cp(tw_al[64:128, :], pl.sub(12, (slice(64, 128), 12)), eng="dve")
                        if c == 13:
                            act(sgl[:, :], pl.sub(13), AF.Sigmoid)

                    chk("rproj")
                    for j in range(4):
                        r_ = pl.sub(j)
                        k_ = pl.sub(4 + j)
                        v_ = pl.sub(8 + j)
                        cs = slice(j * 128, (j + 1) * 128)
                        ps = PF()
                        chk("q0")
                        mm(ps[:, 0:128], lorab[0:64, cs], tw_al[0:64, :])
                        chk("q1")
                        psL2 = PF()
                        mm(psL2[:, 0:128], lorab[64:128, cs], tw_al[64:128, :])
                        chk("q2")
                        mm(ps[:, 256:384], g2b[:, cs], sgl[:, :])
                        chk("q3")
                        lw, asig, cw_, cwp, ew, einv, ehat, eprev, kq, kkn, kpr, bvec, t0_, t1_ = tf
                        act(lw[:, :], ps[:, 0:128], AF.Sigmoid, bias=pvv("w0", j))
                        chk("q4")
                        ts(lw[:, :], lw[:, :], -0.6065306597126334, ALU.mult)
                        chk("q5")
                        act(asig[:, :], psL2[:, 0:128], AF.Sigmoid, bias=pvv("a0", j))
                        chk("q6")
                        cp(gT.sub(j), ps[:, 256:384], eng="act")
                        chk("r1")
                        scan_cumsum(cw_[:, :], C("ones"), lw[:, :])
                        tt(cwp[:, :], cw_[:, :], lw[:, :], ALU.subtract)
                        act(ew[:, :], cw_[:, :], AF.Exp)
                        act(einv[:, :], cw_[:, :], AF.Exp, scale=-1.0)
                        act(ehat[:, :], cw_[:, :], AF.Exp, scale=-1.0, bias=cw_[:, 127:128])
                        act(eprev[:, :], cwp[:, :], AF.Exp)
                        chk("r2")
                        ts(kq[:, :], k_, pvv("kk", j), ALU.mult)
                        act(t0_[:, :], kq[:, :], AF.Square)
                        ps2 = PF()
                        mm(ps2[:, 0:128], C("blk"), t0_[:, :])
                        act(t1_[:, :], ps2[:, 0:128], AF.Sqrt, bias=1e-6)
                        recip(t1_[:, :], t1_[:, :])
                        tt(kkn[:, :], kq[:, :], t1_[:, :], ALU.mult)
                        chk("r3")
                        ts(t0_[:, :], asig[:, :], pvv("ka", j), ALU.mult, dv[:, j:j + 1], ALU.add)
                        tt(kpr[:, :], k_, t0_[:, :], ALU.mult)
                        tt(bvec[:, :], kkn[:, :], asig[:, :], ALU.mult)
                        stt(ar.sub(j, (slice(None), j, slice(0, 128))), kkn[:, :], -1.0, eprev[:, :], ALU.mult, ALU.mult)
                        tt(ar.sub(j, (slice(None), j, slice(128, 256))), r_, ew[:, :], ALU.mult)
                        tt(btl.sub(j), bvec[:, :], einv[:, :], ALU.mult)
                        tt(ktl.sub(j), kpr[:, :], einv[:, :], ALU.mult)
                        bh, kh, vb_ = tb[0], tb[1], tb[2]
                        tt(bh[:, :], bvec[:, :], ehat[:, :], ALU.mult)
                        tt(kh[:, :], kpr[:, :], ehat[:, :], ALU.mult)
                        cp(vb_[:, :], v_, eng="act")
                        stt(t0_[:, :], r_, pvv("rk", j), kpr[:, :], ALU.mult, ALU.mult)
                        mm(ps2[:, 128:256], C("blk"), t0_[:, :])
                        tt(bon.sub(j), ps2[:, 128:256], v_, ALU.mult)
                        chk("r4")
                        pb = PB()
                        tr(pb[:, 0:128], vb_[:, :], identb[:, :])
                        chk("t1")
                        tr(pb[:, 128:256], ar.sub(j, (slice(None), j, slice(0, 128))), identb[:, :])
                        chk("t2")
                        tr(pb[:, 256:384], bh[:, :], identb[:, :])
                        tr(pb[:, 384:512], kh[:, :], identb[:, :])
                        chk("t4")
                        cp(tokm.sub(j), pb[:, 0:512], eng="act")
                        chk("t5")
                        cp(vpad[0].sub(j, (slice(None), j, slice(0, 64))), tokm.sub(j, (slice(None), j, slice(0, 64))), eng="dve")
                        chk("t6")
                        cp(vpad[1].sub(j, (slice(None), j, slice(64, 128))), tokm.sub(j, (slice(None), j, slice(64, 128))), eng="dve")
                        chk("r5")
                        sysl = []
                        for hh in range(2):
                            prow = slice(hh * 64, hh * 64 + 64)
                            psA = PF()
                            mm(psA[:, 0:256], btl.sub(j, (prow, j)), ar.sub(j, (prow, j)))
                            mm(psA[:, 256:512], ktl.sub(j, (prow, j)), ar.sub(j, (prow, j)))
                            psB = PF()
                            mm(psB[:, 0:128], ar.sub(j, (prow, j, slice(0, 128))), btl.sub(j, (prow, j)))
                            tt(AM[hh][:, :], psA[:, 0:512], C("su", 4), ALU.mult)
                            bA, bB = NTP[2 * hh], NTP[2 * hh + 1]
                            tt(bA[:, 0:128], psB[:, 0:128], C("sl"), ALU.mult)
                            t0f = cwp if hh == 0 else einv
                            tt(t0f[:, :], psA[:, 0:128], C("su"), ALU.mult)
                            sysl.append((bA[:, 0:128], t0f[:, :], tinvT[hh][:, :], bA, bB))
                        neumann_batch(sysl)
                        chk("r6")
                        psX = PF()
                        mm(psX[:, 0:128], ar.sub(j, (slice(None), j, slice(0, 128))), Hbd.sub(j), start=True, stop=False)
                        mm(psX[:, 0:128], AM[0][:, 256:384], vpad[0].sub(j), start=False, stop=False)
                        mm(psX[:, 0:128], AM[1][:, 256:384], vpad[1].sub(j), start=False, stop=True)
                        cp(Xb[:, :], psX[:, 0:128], eng="act")
                        psU = PF()
                        mm(psU[:, 0:64], tinvT[0][:, :], Xb[:, 0:64])
                        mm(psU[:, 64:128], tinvT[1][:, :], Xb[:, 64:128])
                        cp(upad[0][:, 0:64], psU[:, 0:64], eng="act")
                        cp(upad[1][:, 64:128], psU[:, 64:128], eng="dve")
                        chk("r7")
                        psY = PF()
                        mm(psY[:, 0:128], Hbd.sub(j), ar.sub(j, (slice(None), j, slice(128, 256))), start=True, stop=False)
                        for hh in range(2):
                            mm(psY[:, 0:128], upad[hh][:, :], AM[hh][:, 128:256], start=False, stop=False)
                            mm(psY[:, 0:128], vpad[hh].sub(j), AM[hh][:, 384:512], start=False, stop=(hh == 1))
                        psH = PF()
                        utile = tb[3]
                        tt(utile[:, :], upad[0][:, :], upad[1][:, :], ALU.add)
                        mm(psH[:, 0:128], tokm.sub(j, (slice(None), j, slice(256, 384))), utile[:, :], start=True, stop=False)
                        mm(psH[:, 0:128], tokm.sub(j, (slice(None), j, slice(384, 512))), tokm.sub(j, (slice(None), j, slice(0, 128))),
                           start=False, stop=True)
                        tt(t0_[:, :], psH[:, 0:128], C("blk"), ALU.mult)
                        stt(H32.sub(j), H32.sub(j), ew[:, 127:128], t0_[:, :], ALU.mult, ALU.add)
                        cp(Hbd.sub(j), H32.sub(j), eng="act")
                        chk("r8")
                        y32, ysq = t1_, kq
                        cp(y32[:, :], psY[:, 0:128], eng="act")
                        act(ysq[:, :], y32[:, :], AF.Square)
                        psG = PF()
                        mm(psG[:, 0:128], C("blkm"), y32[:, :])
                        mm(psG[:, 128:256], C("blkm"), ysq[:, :])
                        msq, var = kkn, kpr
                        act(msq[:, :], psG[:, 0:128], AF.Square)
                        tt(var[:, :], psG[:, 128:256], msq[:, :], ALU.subtract)
                        act(var[:, :], var[:, :], AF.Sqrt, bias=64e-5)
                        recip(var[:, :], var[:, :])
                        tt(y32[:, :], y32[:, :], psG[:, 0:128], ALU.subtract)
                        tt(y32[:, :], y32[:, :], var[:, :], ALU.mult)
                        ts(y32[:, :], y32[:, :], pvv("lnw", j), ALU.mult, pvv("lnb", j), ALU.add)
                        tt(y32[:, :], y32[:, :], bon.sub(j), ALU.add)
                        tt(yfin.sub(j), y32[:, :], gT.sub(j), ALU.mult)

                    chk("rwkv")
                    psg = PF()
                    for kc in range(8):
                        mm(psg[:, 0:8], uT[:, kc, :], winb[:, kc, 3840:3848], start=(kc == 0), stop=(kc == 7))
                    tt(gt[:, 0:4], psg[:, 0:4], pvv("dtb"), ALU.add)
                    act(gt[:, 4:8], psg[:, 4:8], AF.Sigmoid)
                    act(gt[:, 0:4], gt[:, 0:4], AF.Exp)
                    act(gt[:, 0:4], gt[:, 0:4], AF.Ln, bias=1.0)
                    tt(gt[:, 0:4], gt[:, 0:4], dv[:, 4:8], ALU.mult)
                    psg2 = PF()
                    mm(psg2[:, 0:4], C("iu"), gt[:, 0:4])
                    mm(psg2[:, 4:8], C("ones"), gt[:, 0:4])
                    cp(gt[:, 8:16], psg2[:, 0:8], eng="act")
                    act(gt[:, 16:20], gt[:, 8:12], AF.Exp)
                    tt(gt[:, 20:24], gt[:, 12:16], gt[:, 8:12], ALU.subtract)
                    act(gt[:, 20:24], gt[:, 20:24], AF.Exp)
                    act(gt[:, 24:28], gt[:, 12:16], AF.Exp)
                    tt(gt[:, 28:32], gt[:, 4:8], gt[:, 16:20], ALU.mult)
                    ts(gt[:, 32:36], gt[:, 8:12], -1.0, ALU.mult)
                    ts(gt[:, 36:40], gt[:, 4:8], -1.0, ALU.mult)

                    chk("gates")
                    for c in range(12):
                        ps = PF()
                        proj_chunk(ps[:, 0:128], 1792 + c * 128)
                        pr = praw[c % 2]
                        cp(pr[:, 3:131], ps[:, 0:128], eng="act")
                        cp(pr[:, 0:3], halo_g[:, c, :], eng="dve")
                        cp(halo_g[:, c, :], pr[:, 128:131], eng="dve")
                        acc = tf[0]
                        o_, _w = PV_OFF["cw"]
                        ts(acc[:, :], pr[:, 0:128], pv[:, o_ + c * 4:o_ + c * 4 + 1], ALU.mult)
                        for tap in range(1, 4):
                            stt(acc[:, :], pr[:, tap:tap + 128], pv[:, o_ + c * 4 + tap:o_ + c * 4 + tap + 1], acc[:, :],
                                ALU.mult, ALU.add)
                        sl_ = tf[1]
                        act(sl_[:, :], acc[:, :], AF.Silu)
                        h = c % 4
                        if c < 8:
                            sq = tf[2]
                            act(sq[:, :], sl_[:, :], AF.Square)
                            psn = PF()
                            mm(psn[:, 0:128], C("ones"), sq[:, :])
                            rn = tf[3]
                            act(rn[:, :], psn[:, 0:128], AF.Sqrt, bias=1e-6)
                            recip(rn[:, :], rn[:, :])
                            if c < 4:
                                stt(qT.sub(h), sl_[:, :], 128.0 ** -0.5, rn[:, :], ALU.mult, ALU.mult)
                            else:
                                tt(kT.sub(h), sl_[:, :], rn[:, :], ALU.mult)
                        else:
                            cp(vTb.sub(h), sl_[:, :], eng="dve")
                    for h in range(4):
                        ps = PF()
                        proj_chunk(ps[:, 0:128], 3328 + h * 128)
                        act(siluz.sub(h), ps[:, 0:128], AF.Silu)

                    chk("gconv")
                    for h in range(4):
                        def g_(o):
                            return gt[:, o + h:o + h + 1]
                        diag, Ds, DTi, o1s, o32 = tf[4], tf[5], tf[6], tf[7], tf[8]
                        ts(diag[:, :], C("ident"), g_(8), ALU.mult)
                        psR = PF()
                        mm(psR[:, 0:128], C("ones"), diag[:, :], start=True, stop=False)
                        mm(psR[:, 0:128], C("ident"), C("mblo"), start=False, stop=True)
                        mm(psR[:, 128:256], C("ones"), diag[:, :], start=True, stop=False)
                        mm(psR[:, 128:256], C("ident"), C("mbup"), start=False, stop=True)
                        act(Ds[:, :], psR[:, 0:128], AF.Exp, scale=-1.0, bias=g_(8))
                        act(DTi[:, :], psR[:, 128:256], AF.Exp, bias=g_(32))
                        psK = PF()
                        mm(psK[:, 0:128], kT.sub(h), kT.sub(h))
                        mm(psK[:, 128:256], kT.sub(h), qT.sub(h))
                        n0 = NTP[0]
                        stt(n0[:, 0:128], psK[:, 0:128], g_(36), Ds[:, :], ALU.mult, ALU.mult)
                        attnT = tb[0]
                        tt(attnT[:, :], psK[:, 128:256], DTi[:, :], ALU.mult)
                        pb = PB()
                        psT = PF()
                        mm(psT[:, 0:128], n0[:, 0:128], C("ident"))
                        tr(pb[:, 128:256], kT.sub(h), identb[:, :])
                        tr(pb[:, 256:384], vTb.sub(h), identb[:, :])
                        t0b, kbd, kdec, vbt, nwk, vn = tb[1], tb[2], tb[3], tb[4], tb[5], Xb
                        t0b = tf[9]
                        cp(t0b[:, :], psT[:, 0:128], eng="act")
                        act(kbd[:, :], pb[:, 128:256], AF.Copy, scale=g_(28))
                        act(kdec[:, :], pb[:, 128:256], AF.Copy, scale=g_(20))
                        act(vbt[:, :], pb[:, 256:384], AF.Copy, scale=g_(4))
                        neumann_batch([(n0[:, 0:128], t0b[:, :], tinvT[0][:, :], NTP[0], NTP[1])])
                        psW = PF()
                        mm(psW[:, 0:128], kbd[:, :], tinvT[0][:, :])
                        ts(nwk[:, :], psW[:, 0:128], -1.0, ALU.mult)
                        psV = PF()
                        mm(psV[:, 0:128], tinvT[0][:, :], vbt[:, :], start=True, stop=False)
                        mm(psV[:, 0:128], nwk[:, :], Sb.sub(h), start=False, stop=True)
                        cp(vn[:, :], psV[:, 0:128], eng="act")
                        psO = PF()
                        mm(psO[:, 0:128], qT.sub(h), Sb.sub(h))
                        mm(psO[:, 128:256], attnT[:, :], vn[:, :])
                        act(o1s[:, :], psO[:, 0:128], AF.Copy, scale=g_(16))
                        tt(o32[:, :], psO[:, 128:256], o1s[:, :], ALU.add)
                        psS = PF()
                        mm(psS[:, 0:128], kdec[:, :], vn[:, :])
                        stt(S32.sub(h), S32.sub(h), g_(24), psS[:, 0:128], ALU.mult, ALU.add)
                        cp(Sb.sub(h), S32.sub(h), eng="act")
                        memset(sc[:, 4:5], 0.0)
                        act(o1s[:, :], o32[:, :], AF.Square, accum=sc[:, 4:5])
                        act(sc[:, 5:6], sc[:, 4:5], AF.Sqrt, bias=1e-6, scale=1.0 / 128)
                        recip(sc[:, 6:7], sc[:, 5:6])
                        onb = tb[1]
                        ts(onb[:, :], o32[:, :], sc[:, 6:7], ALU.mult)
                        pb2 = PB()
                        tr(pb2[:, 0:128], onb[:, :], identb[:, :])
                        act(o1s[:, :], pb2[:, 0:128], AF.Copy, scale=pvv("nw"))
                        tt(ybT.sub(h), o1s[:, :], siluz.sub(h), ALU.mult)

                    chk("gdn")
                    for oc in range(8):
                        osl = slice(oc * 128, (oc + 1) * 128)
                        ps = PF()
                        for jj in range(4):
                            mm(ps[:, 0:128], rprojb[:, jj, osl], yfin.sub(jj), start=(jj == 0), stop=(jj == 3))
                        for jj in range(4):
                            mm(ps[:, 128:256], gprojb[:, jj, osl], ybT.sub(jj), start=(jj == 0), stop=(jj == 3))
                        proj_chunk(ps[:, 256:384], 3848 + oc * 128)
                        proj_chunk(ps[:, 384:512], 4872 + oc * 128)
                        sga, sgb = tf[0], tf[1]
                        act(sga[:, :], ps[:, 256:384], AF.Sigmoid)
                        act(sgb[:, :], ps[:, 384:512], AF.Sigmoid)
                        tt(sga[:, :], sga[:, :], ps[:, 0:128], ALU.mult)
                        tt(sgb[:, :], sgb[:, :], ps[:, 128:256], ALU.mult)
                        tt(mixT.sub(oc), sga[:, :], sgb[:, :], ALU.add)

                    for nb in range(2):
                        ps = PF()
                        for kc in range(8):
                            mm(ps[:, 0:512], mixT.sub(kc), woutb[:, kc, nb * 512:(nb + 1) * 512], start=(kc == 0), stop=(kc == 7))
                        tt(x1[:, nb * 512:(nb + 1) * 512], ps[:, 0:512], xt[:, nb * 512:(nb + 1) * 512], ALU.add)
                    S.dma("sp", x1_d[it * 128:(it + 1) * 128, :], x1[:, :], sembuf=x1.bufs[0])

                S.barrier()
            chk("phase1")
            S.stack = gstack

            with ExitStack() as st2:
                S.stack = st2
                upb = Tile(S, "upb", [128, 8, 2 * FH], BF16)
                downb = Tile(S, "downb", [128, 22, D], BF16)
                fgb = Tile(S, "fgb", [128, D], F32)
                with ExitStack() as stg_stack:
                    S.stack = stg_stack
                    stg = [Tile(S, f"stgb{i}", [128, 1536], F32) for i in range(2)]
                    k = [0]

                    def load_cast2(dst, src, ncols, scale=None):
                        s = stg[k[0] % 2]
                        S.dma("sp", s[:, 0:ncols], src)
                        if scale is not None:
                            if k[0] % 2 == 0:
                                act(dst, s[:, 0:ncols], AF.Copy, scale=scale)
                            else:
                                ts(dst, s[:, 0:ncols], scale, ALU.mult)
                        else:
                            cp(dst, s[:, 0:ncols], eng="act" if k[0] % 2 == 0 else "dve")
                        k[0] += 1

                    CB2 = 1408
                    for kc in range(8):
                        for cb in range(4):
                            load_cast2(upb[:, kc, cb * CB2:(cb + 1) * CB2], up_d[kc * 128:(kc + 1) * 128, cb * CB2:(cb + 1) * CB2],
                                       CB2, scale=pvv("g2n", kc))
                    for kc in range(22):
                        load_cast2(downb[:, kc, :], down_d[kc * 128:(kc + 1) * 128, :], D)
                    S.dma("sp", fgb[:, :], View(fg_d.ap.partition_broadcast(128), [fg_d.buf]))
                    S.barrier()
                S.stack = st2

                chk("setup2")
                x1t = Tile(S, "x1t", [128, D], F32)
                sc2 = Tile(S, "sc2", [128, 16], F32)
                junk2 = Tile(S, "junk2", [128, D], BF16)
                u2 = Tile(S, "u2", [128, D], BF16)
                u2T = Tile(S, "u2T", [128, 8, 128], BF16)
                halo_f = Tile(S, "halo_f", [128, 44, 2], F32)
                hb = [Tile(S, f"hb{i}", [128, 2, 130], F32) for i in range(2)]
                hc = [Tile(S, f"hc{i}", [128, 2, 128], F32) for i in range(2)]
                actT = Tile(S, "actT", [128, 22, 128], BF16, split=True)
                x2 = Tile(S, "x2", [128, D], F32)
                ot = Tile(S, "ot", [128, D], F32)
                memset(halo_f[:, :, :], 0.0)
                fo, _ = PV_OFF["fcw"]

                for it in range(NT):
                    S.dma("sp", x1t[:, :], x1_d[it * 128:(it + 1) * 128, :])
                    memset(sc2[:, 0:1], 0.0)
                    act(junk2[:, :], x1t[:, :], AF.Square, accum=sc2[:, 0:1])
                    act(sc2[:, 1:2], sc2[:, 0:1], AF.Sqrt, bias=1e-6, scale=1.0 / D)
                    recip(sc2[:, 2:3], sc2[:, 1:2])
                    ts(u2[:, :], x1t[:, :], sc2[:, 2:3], ALU.mult)
                    pb = PB()
                    for kc in range(8):
                        tr(pb[:, kc * 128:(kc + 1) * 128], u2[:, kc * 128:(kc + 1) * 128], identb[:, :])
                    S.op("act", lambda e: e.activation(out=u2T.t[:, :, :], in_=pb.t[:, 0:1024].rearrange("p (a b) -> p a b", a=8), func=AF.Copy),
                         [u2T[:, :, :]], [pb[:, :]])
                    chk("norm2")
                    for c in range(22):
                        ps = PF()
                        for half, cc in enumerate((c, c + 22)):
                            for kc in range(8):
                                mm(ps[:, half * 128:(half + 1) * 128], upb[:, kc, cc * 128:(cc + 1) * 128], u2T[:, kc, :],
                                   start=(kc == 0), stop=(kc == 7))
                        hbt = hb[c % 2]
                        hct = hc[c % 2]
                        for half, cc in enumerate((c, c + 22)):
                            cp(hbt[:, half, 2:130], ps[:, half * 128:(half + 1) * 128], eng="act")
                            cp(hbt[:, half, 0:2], halo_f[:, cc, :], eng="dve")
                            cp(halo_f[:, cc, :], hbt[:, half, 128:130], eng="dve")
                            ts(hct[:, half, :], hbt[:, half, 0:128], pv[:, fo + cc * 3:fo + cc * 3 + 1], ALU.mult)
                            for tap in range(1, 3):
                                stt(hct[:, half, :], hbt[:, half, tap:tap + 128], pv[:, fo + cc * 3 + tap:fo + cc * 3 + tap + 1],
                                    hct[:, half, :], ALU.mult, ALU.add)
                        act(hct[:, 0, :], hct[:, 0, :], AF.Silu)
                        tt(actT.sub(c), hct[:, 0, :], hct[:, 1, :], ALU.mult)
                    chk("up")
                    for nb in range(2):
                        ps = PF()
                        for c in range(22):
                            mm(ps[:, 0:512], actT.sub(c), downb[:, c, nb * 512:(nb + 1) * 512], start=(c == 0), stop=(c == 21))
                        tt(x2[:, nb * 512:(nb + 1) * 512], ps[:, 0:512], x1t[:, nb * 512:(nb + 1) * 512], ALU.add)
                    chk("down")
                    memset(sc2[:, 4:5], 0.0)
                    act(junk2[:, :], x2[:, :], AF.Square, accum=sc2[:, 4:5])
                    act(sc2[:, 5:6], sc2[:, 4:5], AF.Sqrt, bias=1e-6, scale=1.0 / D)
                    recip(sc2[:, 6:7], sc2[:, 5:6])
                    stt(ot[:, :], x2[:, :], sc2[:, 6:7], fgb[:, :], ALU.mult, ALU.mult)
                    chk("fin")
                    S.dma("sp", y_d[it * 128:(it + 1) * 128, :], ot[:, :], sembuf=ot.bufs[0])
                S.barrier()
            S.stack = gstack
        except StopBuild:
            pass
        S.stack = gstack
        S.dead = False
        S.barrier()
        build.ninst = S.ninst
    return nc


def make_in_maps(inp, T, ncores):
    f = lambda a: np.ascontiguousarray(np.asarray(a, np.float32))
    pvh = make_pv(inp)
    consts = make_consts()
    shared = {
        "w_in": f(inp["w_in"][0]), "w2": f(inp["rwkv_w2"][0]), "a2": f(inp["rwkv_a2"][0]), "g2": f(inp["rwkv_g2"][0]),
        "rproj": f(inp["rwkv_proj"][0]), "gproj": f(inp["gdn_proj"][0]), "wout": f(inp["w_out"][0]),
        "ffn_up": f(inp["ffn_up"][0]), "ffn_down": f(inp["ffn_down"][0]), "final_g": f(inp["final_g"]),
        "pv": pvh, "consts": consts,
    }
    x = np.asarray(inp["x"], np.float32)
    maps = []
    for c in range(ncores):
        m = dict(shared)
        m["x"] = np.ascontiguousarray(x[c, :T])
        maps.append(m)
    return maps


_NC_CACHE = {}


def kernel(**inputs):
    x = np.asarray(inputs["x"])
    B, T, _ = x.shape
    if T not in _NC_CACHE:
        _NC_CACHE[T] = build(T)
    nc = _NC_CACHE[T]
    maps = make_in_maps(inputs, T, B)
    res = run_bass_kernel_spmd(nc, maps, core_ids=list(range(B)))
    out = np.stack([np.asarray(r["y"], np.float32) for r in res.results], axis=0)
    return out
```

```python
from contextlib import ExitStack
import numpy as np
import concourse.bass as bass
import concourse.mybir as mybir
from concourse.bass_utils import run_bass_kernel_spmd

F32 = mybir.dt.float32
BF16 = mybir.dt.bfloat16
AF = mybir.ActivationFunctionType
ALU = mybir.AluOpType

D = 1024
INW = 5896
FH = 2816
BIG = 32768.0
SEM_LIMIT = 30000


class Buf:
    __slots__ = ("name", "w", "r", "dsem", "dcnt")

    def __init__(self, name):
        self.name = name
        self.w = None
        self.r = {}
        self.dsem = None
        self.dcnt = 0


class View:
    __slots__ = ("ap", "bufs")

    def __init__(self, ap, bufs):
        self.ap = ap
        self.bufs = bufs


class Tile:
    def __init__(self, sched, name, shape, dtype, space="sbuf", split=False):
        nc = sched.nc
        if space == "sbuf":
            self.t = sched.stack.enter_context(nc.sbuf_tensor("t_" + name, list(shape), dtype))
        else:
            self.t = sched.stack.enter_context(nc.psum_tensor("t_" + name, list(shape), dtype))
        self.name = name
        self.split = split
        if split:
            self.bufs = [Buf(f"{name}.{i}") for i in range(shape[1])]
        else:
            self.bufs = [Buf(name)]

    def __getitem__(self, key):
        return View(self.t[key], self.bufs)

    def sub(self, i, key=None):
        ap = self.t[:, i] if key is None else self.t[key]
        return View(ap, [self.bufs[i]] if self.split else self.bufs)


class DramT:
    def __init__(self, ap, name):
        self.ap = ap
        self.buf = Buf(name)

    def __getitem__(self, key):
        return View(self.ap[key], [self.buf])


class EngState:
    def __init__(self, name, eng):
        self.name = name
        self.eng = eng
        self.sem = None
        self.cnt = 0
        self.pending = False
        self.seen = {}


class Sched:
    def __init__(self, nc, stack):
        self.nc = nc
        self.stack = stack
        self.semstack = stack
        self.nsem = 0
        self.E = {}
        for n, e in (("pe", nc.tensor), ("act", nc.scalar), ("dve", nc.vector), ("pool", nc.gpsimd), ("sp", nc.sync)):
            es = EngState(n, e)
            es.sem = self.new_sem(n)
            self.E[n] = es
        self.all_dsems = []
        self.ninst = 0
        self.dead = False

    def new_sem(self, name):
        self.nsem += 1
        return self.semstack.enter_context(self.nc.semaphore(f"{name}_{self.nsem}"))

    def _need(self, reads, writes):
        need = {}

        def add(ev):
            if ev is None:
                return
            s, v = ev
            k = id(s)
            if k not in need or need[k][1] < v:
                need[k] = (s, v)

        for b in reads:
            add(b.w)
        for b in writes:
            add(b.w)
            for k, (s, v) in b.r.items():
                add((s, v))
        return need

    def _wait(self, es, need, skip_self=True):
        for k, (s, v) in need.items():
            if skip_self and s is es.sem:
                continue
            if es.seen.get(k, 0) >= v:
                continue
            es.eng.wait_ge(s, v)
            es.seen[k] = v

    @staticmethod
    def _bufs(views):
        out = []
        for v in views:
            if isinstance(v, View):
                out.extend(v.bufs)
        return out

    def op(self, en, fn, outs, ins, inc=True, order_self=False):
        if self.dead:
            return
        es = self.E[en]
        R = self._bufs(ins)
        W = self._bufs(outs)
        need = self._need(R, W)
        self._wait(es, need, skip_self=(en == "pe"))
        ins_ = fn(es.eng)
        self.ninst += 1
        if es.cnt >= SEM_LIMIT and inc and not es.pending:
            es.sem = self.new_sem(es.name)
            es.cnt = 0
        if inc:
            es.cnt += 1
            ins_.then_inc(es.sem, 1)
            ev = (es.sem, es.cnt)
            es.pending = False
        else:
            ev = (es.sem, es.cnt + 1)
            es.pending = True
        k = id(ev[0])
        for b in R:
            if k not in b.r or b.r[k][1] < ev[1]:
                b.r[k] = ev
        for b in W:
            b.w = ev
            b.r = {}

    def dma(self, qn, out, in_, sembuf=None):
        if self.dead:
            return
        es = self.E[qn]
        R = self._bufs([in_])
        W = self._bufs([out])
        need = self._need(R, W)
        self._wait(es, need, skip_self=False)
        sb = sembuf if sembuf is not None else (W[0] if W else R[0])
        if sb.dsem is None:
            sb.dsem = self.new_sem("d_" + sb.name.replace(".", "_"))
            self.all_dsems.append(sb)
        es.eng.dma_start(out=out.ap, in_=in_.ap).then_inc(sb.dsem, 16)
        self.ninst += 1
        sb.dcnt += 16
        ev = (sb.dsem, sb.dcnt)
        k = id(ev[0])
        for b in R:
            if k not in b.r or b.r[k][1] < ev[1]:
                b.r[k] = ev
        for b in W:
            b.w = ev
            b.r = {}

    def barrier(self):
        if self.dead:
            return
        for es in self.E.values():
            for fs in self.E.values():
                if fs is es or fs.cnt == 0:
                    continue
                if es.seen.get(id(fs.sem), 0) < fs.cnt:
                    es.eng.wait_ge(fs.sem, fs.cnt)
                    es.seen[id(fs.sem)] = fs.cnt
            for sb in self.all_dsems:
                if es.seen.get(id(sb.dsem), 0) < sb.dcnt:
                    es.eng.wait_ge(sb.dsem, sb.dcnt)
                    es.seen[id(sb.dsem)] = sb.dcnt


def A(v):
    return v.ap if isinstance(v, View) else v


PV_FIELDS = [("g1", 8), ("mu", 14), ("w0", 4), ("a0", 4), ("kk", 4), ("ka", 4), ("rk", 4), ("lnw", 4),
             ("lnb", 4), ("cw", 48), ("nw", 1), ("g2n", 8), ("fcw", 132), ("alog", 4), ("dtb", 4)]
PV_OFF = {}
_o = 0
for _n, _w in PV_FIELDS:
    PV_OFF[_n] = (_o, _w)
    _o += _w
PV_W = _o

CONST_NAMES = ["ident", "ones", "su", "iu", "su2", "iu2", "sl", "mblo", "mbup", "blk", "blkm"]
CI = {n: i for i, n in enumerate(CONST_NAMES)}
NCONST = len(CONST_NAMES)


def make_consts():
    i = np.arange(128)
    s = i[:, None]
    t = i[None, :]
    ident = (s == t).astype(np.float32)
    ones = np.ones((128, 128), np.float32)
    su = (s < t).astype(np.float32)
    iu = (s <= t).astype(np.float32)
    sl = (t < s).astype(np.float32)
    mblo = np.where(t >= s, BIG, 0.0).astype(np.float32)
    mbup = np.where(t < s, -BIG, 0.0).astype(np.float32)
    blk = ((s // 64) == (t // 64)).astype(np.float32)
    blkm = blk / 64.0
    return np.concatenate([ident, ones, su, iu, su, iu, sl, mblo, mbup, blk, blkm], axis=1)


def chunked(v, n):
    return np.ascontiguousarray(np.asarray(v, np.float32).reshape(n, 128).T)


def make_pv(inp):
    pv = np.zeros((128, PV_W), np.float32)

    def put(name, arr):
        o, w = PV_OFF[name]
        assert arr.shape == (128, w), (name, arr.shape)
        pv[:, o:o + w] = arr

    put("g1", chunked(inp["norm1_g"][0], 8))
    put("mu", chunked(inp["rwkv_mu"][0], 14))
    put("w0", chunked(inp["rwkv_w0"][0], 4))
    put("a0", chunked(inp["rwkv_a0"][0], 4))
    put("kk", chunked(inp["rwkv_k_k"][0], 4))
    put("ka", chunked(inp["rwkv_k_a"][0], 4))
    put("rk", chunked(inp["rwkv_r_k"][0].reshape(-1), 4))
    put("lnw", chunked(inp["rwkv_ln_w"][0], 4))
    put("lnb", chunked(inp["rwkv_ln_b"][0], 4))
    cw = np.asarray(inp["gdn_conv_w"][0], np.float32)
    put("cw", np.ascontiguousarray(cw.reshape(4, 12, 128).transpose(2, 1, 0).reshape(128, 48)))
    put("nw", np.asarray(inp["gdn_norm_w"][0], np.float32).reshape(128, 1))
    put("g2n", chunked(inp["norm2_g"][0], 8))
    fcw = np.asarray(inp["ffn_conv_w"][0], np.float32)
    put("fcw", np.ascontiguousarray(fcw.reshape(3, 44, 128).transpose(2, 1, 0).reshape(128, 132)))
    put("alog", np.broadcast_to(np.asarray(inp["gdn_a_log"][0], np.float32)[None, :], (128, 4)))
    put("dtb", np.broadcast_to(np.asarray(inp["gdn_dt_bias"][0], np.float32)[None, :], (128, 4)))
    return pv


class StopBuild(Exception):
    pass


def build(T, stop=None):
    NT = T // 128

    def chk(name):
        if stop == name:
            chk.S.dead = True
    nc = bass.Bass("TRN2", target_bir_lowering=False)

    def dram_in(name, shape):
        return DramT(nc.dram_tensor(name, list(shape), F32, kind="ExternalInput").ap(), name)

    x_d = dram_in("x", [T, D])
    win_d = dram_in("w_in", [D, INW])
    w2_d = dram_in("w2", [64, 512])
    a2_d = dram_in("a2", [64, 512])
    g2_d = dram_in("g2", [128, 512])
    rproj_d = dram_in("rproj", [512, D])
    gproj_d = dram_in("gproj", [512, D])
    wout_d = dram_in("wout", [D, D])
    up_d = dram_in("ffn_up", [D, 2 * FH])
    down_d = dram_in("ffn_down", [FH, D])
    fg_d = dram_in("final_g", [D])
    pv_d = dram_in("pv", [128, PV_W])
    c_d = dram_in("consts", [128, NCONST * 128])
    y_d = DramT(nc.dram_tensor("y", [T, D], F32, kind="ExternalOutput").ap(), "y")
    x1_d = DramT(nc.dram_tensor("x1s", [T, D], F32, kind="Internal").ap(), "x1s")

    with ExitStack() as gstack:
        S = Sched(nc, gstack)
        chk.S = S

        def PE(fn, outs, ins, inc=True):
            S.op("pe", fn, outs, ins, inc=inc)

        def mm(out, lhsT, rhs, start=True, stop=True):
            S.op("pe", lambda e: e.matmul(A(out), lhsT=A(lhsT), rhs=A(rhs), start=start, stop=stop),
                 [out], [lhsT, rhs], inc=stop)

        def tr(out, in_, ident):
            S.op("pe", lambda e: e.transpose(A(out), A(in_), A(ident)), [out], [in_, ident])

        def act(out, in_, func, bias=None, scale=None, accum=None):
            kw = {}
            ins = [in_]
            if bias is not None:
                kw["bias"] = A(bias)
                ins.append(bias)
            if scale is not None:
                kw["scale"] = A(scale)
                ins.append(scale)
            outs = [out]
            if accum is not None:
                kw["accum_out"] = A(accum)
                outs.append(accum)
            S.op("act", lambda e: e.activation(out=A(out), in_=A(in_), func=func, **kw), outs, ins)

        def tt(out, a, b, op, eng="dve"):
            S.op(eng, lambda e: e.tensor_tensor(out=A(out), in0=A(a), in1=A(b), op=op), [out], [a, b],
                 order_self=(eng == "pool"))

        def ts(out, a, s1, op0, s2=None, op1=None, eng="dve"):
            if op1 is None:
                S.op(eng, lambda e: e.tensor_scalar(out=A(out), in0=A(a), scalar1=A(s1), scalar2=None, op0=op0),
                     [out], [a, s1], order_self=(eng == "pool"))
            else:
                S.op(eng, lambda e: e.tensor_scalar(out=A(out), in0=A(a), scalar1=A(s1), scalar2=A(s2), op0=op0, op1=op1),
                     [out], [a, s1, s2], order_self=(eng == "pool"))

        def stt(out, a, s, b, op0, op1, eng="dve"):
            S.op(eng, lambda e: e.scalar_tensor_tensor(out=A(out), in0=A(a), scalar=A(s), in1=A(b), op0=op0, op1=op1),
                 [out], [a, s, b], order_self=(eng == "pool"))

        def cp(out, in_, eng="dve"):
            if eng == "act":
                act(out, in_, AF.Copy)
            else:
                S.op(eng, lambda e: e.tensor_copy(out=A(out), in_=A(in_)), [out], [in_], order_self=(eng == "pool"))

        def memset(out, val, eng="dve"):
            S.op(eng, lambda e: e.memset(A(out), val), [out], [], order_self=(eng == "pool"))

        def recip(out, in_):
            S.op("dve", lambda e: e.reciprocal(out=A(out), in_=A(in_)), [out], [in_])

        def scan_cumsum(out, ones, data):
            S.op("dve", lambda e: e.tensor_tensor_scan(out=A(out), data0=A(ones), data1=A(data), initial=0.0,
                                                       op0=ALU.mult, op1=ALU.add), [out], [ones, data])

        c32 = Tile(S, "c32", [128, NCONST * 128], F32)
        pv = Tile(S, "pv", [128, PV_W], F32)
        identb = Tile(S, "identb", [128, 128], BF16)
        dv = Tile(S, "dv", [128, 16], F32)
        psf = [Tile(S, f"psf{i}", [128, 512], F32, space="psum") for i in range(6)]
        psb = [Tile(S, f"psb{i}", [128, 1024], BF16, space="psum") for i in range(2)]
        rr = {"f": 0, "b": 0}

        def PF():
            t = psf[rr["f"] % 6]
            rr["f"] += 1
            return t

        def PB():
            t = psb[rr["b"] % 2]
            rr["b"] += 1
            return t

        def C(name, w=1):
            i = CI[name]
            return c32[:, i * 128:(i + w) * 128]

        def pvv(name, j=None, w=1):
            o, wd = PV_OFF[name]
            if j is None:
                return pv[:, o:o + wd]
            return pv[:, o + j:o + j + w]

        S.dma("sp", c32[:, :], c_d[:, :])
        S.dma("sp", pv[:, :], pv_d[:, :])

        try:
            cp(identb[:, :], C("ident"))
            ts(dv[:, 0:4], pvv("ka"), -1.0, ALU.mult, 1.0, ALU.add)
            act(dv[:, 4:8], pvv("alog"), AF.Exp)
            ts(dv[:, 4:8], dv[:, 4:8], -1.0, ALU.mult)

            with ExitStack() as st1:
                S.stack = st1
                winb = Tile(S, "winb", [128, 8, INW], BF16)
                rprojb = Tile(S, "rprojb", [128, 4, D], BF16)
                gprojb = Tile(S, "gprojb", [128, 4, D], BF16)
                woutb = Tile(S, "woutb", [128, 8, D], BF16)
                lorab = Tile(S, "lorab", [128, 512], BF16)
                g2b = Tile(S, "g2b", [128, 512], BF16)

                with ExitStack() as stg_stack:
                    S.stack = stg_stack
                    stg = [Tile(S, f"stg{i}", [128, 1536], F32) for i in range(2)]
                    k = [0]

                    def load_cast(dst, src, ncols, scale=None, prow=slice(0, 128)):
                        s = stg[k[0] % 2]
                        S.dma("sp", s[prow, 0:ncols], src)
                        if scale is not None:
                            if k[0] % 2 == 0:
                                act(dst, s[prow, 0:ncols], AF.Copy, scale=scale)
                            else:
                                ts(dst, s[prow, 0:ncols], scale, ALU.mult)
                        else:
                            cp(dst, s[prow, 0:ncols], eng="act" if k[0] % 2 == 0 else "dve")
                        k[0] += 1

                    CB = 1474
                    for kc in range(8):
                        for cb in range(4):
                            load_cast(winb[:, kc, cb * CB:(cb + 1) * CB], win_d[kc * 128:(kc + 1) * 128, cb * CB:(cb + 1) * CB],
                                      CB, scale=pvv("g1", kc))
                    for kc in range(4):
                        load_cast(rprojb[:, kc, :], rproj_d[kc * 128:(kc + 1) * 128, :], D)
                        load_cast(gprojb[:, kc, :], gproj_d[kc * 128:(kc + 1) * 128, :], D)
                    for kc in range(8):
                        load_cast(woutb[:, kc, :], wout_d[kc * 128:(kc + 1) * 128, :], D)
                    load_cast(lorab[0:64, :], w2_d[:, :], 512, prow=slice(0, 64))
                    load_cast(lorab[64:128, :], a2_d[:, :], 512, prow=slice(64, 128))
                    load_cast(g2b[:, :], g2_d[:, :], 512)
                    S.barrier()
                S.stack = st1

                chk("setup1")
                xt = Tile(S, "xt", [128, D], F32)
                sc = Tile(S, "sc", [128, 16], F32)
                junk = Tile(S, "junk", [128, D], BF16)
                ub = Tile(S, "ub", [128, D], BF16)
                uT = Tile(S, "uT", [128, 8, 128], BF16)
                halo_r = Tile(S, "halo_r", [128, 14], F32)
                praw = [Tile(S, f"praw{i}", [128, 132], F32) for i in range(2)]
                pl = Tile(S, "pl", [128, 14, 128], F32, split=True)
                tw_al = Tile(S, "tw_al", [128, 128], BF16)
                sgl = Tile(S, "sgl", [128, 128], BF16)
                gT = Tile(S, "gT", [128, 4, 128], F32, split=True)
                bon = Tile(S, "bon", [128, 4, 128], F32, split=True)
                NF = 14
                tf = [Tile(S, f"tf{i}", [128, 128], F32) for i in range(NF)]
                ar = Tile(S, "ar", [128, 4, 256], BF16, split=True)
                btl = Tile(S, "btl", [128, 4, 128], BF16, split=True)
                ktl = Tile(S, "ktl", [128, 4, 128], BF16, split=True)
                tb = [Tile(S, f"tb{i}", [128, 128], BF16) for i in range(6)]
                tokm = Tile(S, "tokm", [128, 4, 512], BF16, split=True)
                vpad = [Tile(S, f"vpad{h}", [128, 4, 128], BF16, split=True) for h in range(2)]
                upad = [Tile(S, f"upad{h}", [128, 128], BF16) for h in range(2)]
                AM = [Tile(S, f"AM{h}", [128, 512], BF16) for h in range(2)]
                NTP = [Tile(S, f"NTP{i}", [128, 384], F32) for i in range(4)]
                tinvT = [Tile(S, f"tinvT{h}", [128, 128], BF16) for h in range(2)]
                Xb = Tile(S, "Xb", [128, 128], BF16)
                yfin = Tile(S, "yfin", [128, 4, 128], BF16, split=True)
                H32 = Tile(S, "H32", [128, 4, 128], F32, split=True)
                Hbd = Tile(S, "Hbd", [128, 4, 128], BF16, split=True)
                S32 = Tile(S, "S32", [128, 4, 128], F32, split=True)
                Sb = Tile(S, "Sb", [128, 4, 128], BF16, split=True)
                halo_g = Tile(S, "halo_g", [128, 12, 3], F32)
                qT = Tile(S, "qT", [128, 4, 128], BF16, split=True)
                kT = Tile(S, "kT", [128, 4, 128], BF16, split=True)
                vTb = Tile(S, "vTb", [128, 4, 128], BF16, split=True)
                siluz = Tile(S, "siluz", [128, 4, 128], F32, split=True)
                gt = Tile(S, "gt", [128, 64], F32)
                ybT = Tile(S, "ybT", [128, 4, 128], BF16, split=True)
                mixT = Tile(S, "mixT", [128, 8, 128], BF16, split=True)
                x1 = Tile(S, "x1", [128, D], F32)

                memset(halo_r[:, :], 0.0)
                memset(halo_g[:, :, :], 0.0)
                memset(H32[:, :, :], 0.0)
                memset(Hbd[:, :, :], 0.0)
                memset(S32[:, :, :], 0.0)
                memset(Sb[:, :, :], 0.0)
                for h in range(2):
                    memset(vpad[h][:, :, :], 0.0)
                    memset(upad[h][:, :], 0.0)

                def proj_chunk(ps_view, col0, ncols=128):
                    for kc in range(8):
                        mm(ps_view, winb[:, kc, col0:col0 + ncols], uT[:, kc, :], start=(kc == 0), stop=(kc == 7))

                def neumann_batch(systems):
                    idb = C("ident")
                    cur = [None] * len(systems)
                    for lvl in range(7):
                        for si, (N0, T0, outT, bA, bB) in enumerate(systems):
                            ps = PF()
                            if lvl == 0:
                                Nk, Tk, Pk = N0, T0, None
                            else:
                                c_ = cur[si]
                                Nk, Tk, Pk = c_[:, 0:128], c_[:, 128:256], c_[:, 256:384]
                            if lvl <= 5:
                                mm(ps[:, 0:128], Tk, Nk)
                            if lvl <= 4:
                                mm(ps[:, 128:256], Nk, Tk)
                            if lvl == 0:
                                mm(ps[:, 256:384], idb, T0, start=True, stop=False)
                                mm(ps[:, 256:384], idb, idb, start=False, stop=True)
                            else:
                                mm(ps[:, 256:384], Nk, Pk, start=True, stop=False)
                                mm(ps[:, 256:384], idb, Pk, start=False, stop=True)
                            e1 = "act" if si % 2 == 0 else "dve"
                            e2 = "dve" if si % 2 == 0 else "act"
                            if lvl == 6:
                                cp(outT, ps[:, 256:384], eng=e1)
                            else:
                                nxt = bB if lvl % 2 == 0 else bA
                                if lvl == 5:
                                    cp(nxt[:, 0:128], ps[:, 0:128], eng=e1)
                                    cp(nxt[:, 256:384], ps[:, 256:384], eng=e2)
                                else:
                                    cp(nxt[:, 0:384], ps[:, 0:384], eng=e1)
                                cur[si] = nxt

                for it in range(NT):
                    S.dma("sp", xt[:, :], x_d[it * 128:(it + 1) * 128, :])
                    memset(sc[:, 0:1], 0.0)
                    act(junk[:, :], xt[:, :], AF.Square, accum=sc[:, 0:1])
                    act(sc[:, 1:2], sc[:, 0:1], AF.Sqrt, bias=1e-6, scale=1.0 / D)
                    recip(sc[:, 2:3], sc[:, 1:2])
                    ts(ub[:, :], xt[:, :], sc[:, 2:3], ALU.mult)
                    pb = PB()
                    for kc in range(8):
                        tr(pb[:, kc * 128:(kc + 1) * 128], ub[:, kc * 128:(kc + 1) * 128], identb[:, :])
                    S.op("act", lambda e: e.activation(out=uT.t[:, :, :], in_=pb.t[:, 0:1024].rearrange("p (a b) -> p a b", a=8), func=AF.Copy),
                         [uT[:, :, :]], [pb[:, :]])

                    chk("norm")
                    for c in [12, 13] + list(range(12)):
                        ps = PF()
                        proj_chunk(ps[:, 0:128], c * 128)
                        pr = praw[c % 2]
                        cp(pr[:, 1:129], ps[:, 0:128], eng="act")
                        cp(pr[:, 0:1], halo_r[:, c:c + 1], eng="dve")
                        cp(halo_r[:, c:c + 1], pr[:, 128:129], eng="dve")
                        d_ = tf[0]
                        tt(d_[:, :], pr[:, 0:128], pr[:, 1:129], ALU.subtract)
                        stt(pl.sub(c), d_[:, :], pvv("mu", c), pr[:, 1:129], ALU.mult, ALU.add)
                        if c == 12:
                            act(tw_al[0:64, :], pl.sub(12, (slice(0, 64), 12)), AF.Tanh)
                            cp(tw_al[64:128, :], pl.sub(12, (slice(64, 128), 12)), eng="dve")
                        if c == 13:
                            act(sgl[:, :], pl.sub(13), AF.Sigmoid)

                    chk("rproj")
                    for j in range(4):
                        r_ = pl.sub(j)
                        k_ = pl.sub(4 + j)
                        v_ = pl.sub(8 + j)
                        cs = slice(j * 128, (j + 1) * 128)
                        ps = PF()
                        chk("q0")
                        mm(ps[:, 0:128], lorab[0:64, cs], tw_al[0:64, :])
                        chk("q1")
                        psL2 = PF()
                        mm(psL2[:, 0:128], lorab[64:128, cs], tw_al[64:128, :])
                        chk("q2")
                        mm(ps[:, 256:384], g2b[:, cs], sgl[:, :])
                        chk("q3")
                        lw, asig, cw_, cwp, ew, einv, ehat, eprev, kq, kkn, kpr, bvec, t0_, t1_ = tf
                        act(lw[:, :], ps[:, 0:128], AF.Sigmoid, bias=pvv("w0", j))
                        chk("q4")
                        ts(lw[:, :], lw[:, :], -0.6065306597126334, ALU.mult)
                        chk("q5")
                        act(asig[:, :], psL2[:, 0:128], AF.Sigmoid, bias=pvv("a0", j))
                        chk("q6")
                        cp(gT.sub(j), ps[:, 256:384], eng="act")
                        chk("r1")
                        scan_cumsum(cw_[:, :], C("ones"), lw[:, :])
                        tt(cwp[:, :], cw_[:, :], lw[:, :], ALU.subtract)
                        act(ew[:, :], cw_[:, :], AF.Exp)
                        act(einv[:, :], cw_[:, :], AF.Exp, scale=-1.0)
                        act(ehat[:, :], cw_[:, :], AF.Exp, scale=-1.0, bias=cw_[:, 127:128])
                        act(eprev[:, :], cwp[:, :], AF.Exp)
                        chk("r2")
                        ts(kq[:, :], k_, pvv("kk", j), ALU.mult)
                        act(t0_[:, :], kq[:, :], AF.Square)
                        ps2 = PF()
                        mm(ps2[:, 0:128], C("blk"), t0_[:, :])
                        act(t1_[:, :], ps2[:, 0:128], AF.Sqrt, bias=1e-6)
                        recip(t1_[:, :], t1_[:, :])
                        tt(kkn[:, :], kq[:, :], t1_[:, :], ALU.mult)
                        chk("r3")
                        ts(t0_[:, :], asig[:, :], pvv("ka", j), ALU.mult, dv[:, j:j + 1], ALU.add)
                        tt(kpr[:, :], k_, t0_[:, :], ALU.mult)
                        tt(bvec[:, :], kkn[:, :], asig[:, :], ALU.mult)
                        stt(ar.sub(j, (slice(None), j, slice(0, 128))), kkn[:, :], -1.0, eprev[:, :], ALU.mult, ALU.mult)
                        tt(ar.sub(j, (slice(None), j, slice(128, 256))), r_, ew[:, :], ALU.mult)
                        tt(btl.sub(j), bvec[:, :], einv[:, :], ALU.mult)
                        tt(ktl.sub(j), kpr[:, :], einv[:, :], ALU.mult)
                        bh, kh, vb_ = tb[0], tb[1], tb[2]
                        tt(bh[:, :], bvec[:, :], ehat[:, :], ALU.mult)
                        tt(kh[:, :], kpr[:, :], ehat[:, :], ALU.mult)
                        cp(vb_[:, :], v_, eng="act")
                        stt(t0_[:, :], r_, pvv("rk", j), kpr[:, :], ALU.mult, ALU.mult)
                        mm(ps2[:, 128:256], C("blk"), t0_[:, :])
                        tt(bon.sub(j), ps2[:, 128:256], v_, ALU.mult)
                        chk("r4")
                        pb = PB()
                        tr(pb[:, 0:128], vb_[:, :], identb[:, :])
                        chk("t1")
                        tr(pb[:, 128:256], ar.sub(j, (slice(None), j, slice(0, 128))), identb[:, :])
                        chk("t2")
                        tr(pb[:, 256:384], bh[:, :], identb[:, :])
                        tr(pb[:, 384:512], kh[:, :], identb[:, :])
                        chk("t4")
                        cp(tokm.sub(j), pb[:, 0:512], eng="act")
                        chk("t5")
                        cp(vpad[0].sub(j, (slice(None), j, slice(0, 64))), tokm.sub(j, (slice(None), j, slice(0, 64))), eng="dve")
                        chk("t6")
                        cp(vpad[1].sub(j, (slice(None), j, slice(64, 128))), tokm.sub(j, (slice(None), j, slice(64, 128))), eng="dve")
                        chk("r5")
                        sysl = []
                        for hh in range(2):
                            prow = slice(hh * 64, hh * 64 + 64)
                            psA = PF()
                            mm(psA[:, 0:256], btl.sub(j, (prow, j)), ar.sub(j, (prow, j)))
                            mm(psA[:, 256:512], ktl.sub(j, (prow, j)), ar.sub(j, (prow, j)))
                            psB = PF()
                            mm(psB[:, 0:128], ar.sub(j, (prow, j, slice(0, 128))), btl.sub(j, (prow, j)))
                            tt(AM[hh][:, :], psA[:, 0:512], C("su", 4), ALU.mult)
                            bA, bB = NTP[2 * hh], NTP[2 * hh + 1]
                            tt(bA[:, 0:128], psB[:, 0:128], C("sl"), ALU.mult)
                            t0f = cwp if hh == 0 else einv
                            tt(t0f[:, :], psA[:, 0:128], C("su"), ALU.mult)
                            sysl.append((bA[:, 0:128], t0f[:, :], tinvT[hh][:, :], bA, bB))
                        neumann_batch(sysl)
                        chk("r6")
                        psX = PF()
                        mm(psX[:, 0:128], ar.sub(j, (slice(None), j, slice(0, 128))), Hbd.sub(j), start=True, stop=False)
                        mm(psX[:, 0:128], AM[0][:, 256:384], vpad[0].sub(j), start=False, stop=False)
                        mm(psX[:, 0:128], AM[1][:, 256:384], vpad[1].sub(j), start=False, stop=True)
                        cp(Xb[:, :], psX[:, 0:128], eng="act")
                        psU = PF()
                        mm(psU[:, 0:64], tinvT[0][:, :], Xb[:, 0:64])
                        mm(psU[:, 64:128], tinvT[1][:, :], Xb[:, 64:128])
                        cp(upad[0][:, 0:64], psU[:, 0:64], eng="act")
                        cp(upad[1][:, 64:128], psU[:, 64:128], eng="dve")
                        chk("r7")
                        psY = PF()
                        mm(psY[:, 0:128], Hbd.sub(j), ar.sub(j, (slice(None), j, slice(128, 256))), start=True, stop=False)
                        for hh in range(2):
                            mm(psY[:, 0:128], upad[hh][:, :], AM[hh][:, 128:256], start=False, stop=False)
                            mm(psY[:, 0:128], vpad[hh].sub(j), AM[hh][:, 384:512], start=False, stop=(hh == 1))
                        psH = PF()
                        utile = tb[3]
                        tt(utile[:, :], upad[0][:, :], upad[1][:, :], ALU.add)
                        mm(psH[:, 0:128], tokm.sub(j, (slice(None), j, slice(256, 384))), utile[:, :], start=True, stop=False)
                        mm(psH[:, 0:128], tokm.sub(j, (slice(None), j, slice(384, 512))), tokm.sub(j, (slice(None), j, slice(0, 128))),
                           start=False, stop=True)
                        tt(t0_[:, :], psH[:, 0:128], C("blk"), ALU.mult)
                        stt(H32.sub(j), H32.sub(j), ew[:, 127:128], t0_[:, :], ALU.mult, ALU.add)
                        cp(Hbd.sub(j), H32.sub(j), eng="act")
                        chk("r8")
                        y32, ysq = t1_, kq
                        cp(y32[:, :], psY[:, 0:128], eng="act")
                        act(ysq[:, :], y32[:, :], AF.Square)
                        psG = PF()
                        mm(psG[:, 0:128], C("blkm"), y32[:, :])
                        mm(psG[:, 128:256], C("blkm"), ysq[:, :])
                        msq, var = kkn, kpr
                        act(msq[:, :], psG[:, 0:128], AF.Square)
                        tt(var[:, :], psG[:, 128:256], msq[:, :], ALU.subtract)
                        act(var[:, :], var[:, :], AF.Sqrt, bias=64e-5)
                        recip(var[:, :], var[:, :])
                        tt(y32[:, :], y32[:, :], psG[:, 0:128], ALU.subtract)
                        tt(y32[:, :], y32[:, :], var[:, :], ALU.mult)
                        ts(y32[:, :], y32[:, :], pvv("lnw", j), ALU.mult, pvv("lnb", j), ALU.add)
                        tt(y32[:, :], y32[:, :], bon.sub(j), ALU.add)
                        tt(yfin.sub(j), y32[:, :], gT.sub(j), ALU.mult)

                    chk("rwkv")
                    psg = PF()
                    for kc in range(8):
                        mm(psg[:, 0:8], uT[:, kc, :], winb[:, kc, 3840:3848], start=(kc == 0), stop=(kc == 7))
                    tt(gt[:, 0:4], psg[:, 0:4], pvv("dtb"), ALU.add)
                    act(gt[:, 4:8], psg[:, 4:8], AF.Sigmoid)
                    act(gt[:, 0:4], gt[:, 0:4], AF.Exp)
                    act(gt[:, 0:4], gt[:, 0:4], AF.Ln, bias=1.0)
                    tt(gt[:, 0:4], gt[:, 0:4], dv[:, 4:8], ALU.mult)
                    psg2 = PF()
                    mm(psg2[:, 0:4], C("iu"), gt[:, 0:4])
                    mm(psg2[:, 4:8], C("ones"), gt[:, 0:4])
                    cp(gt[:, 8:16], psg2[:, 0:8], eng="act")
                    act(gt[:, 16:20], gt[:, 8:12], AF.Exp)
                    tt(gt[:, 20:24], gt[:, 12:16], gt[:, 8:12], ALU.subtract)
                    act(gt[:, 20:24], gt[:, 20:24], AF.Exp)
                    act(gt[:, 24:28], gt[:, 12:16], AF.Exp)
                    tt(gt[:, 28:32], gt[:, 4:8], gt[:, 16:20], ALU.mult)
                    ts(gt[:, 32:36], gt[:, 8:12], -1.0, ALU.mult)
                    ts(gt[:, 36:40], gt[:, 4:8], -1.0, ALU.mult)

                    chk("gates")
                    for c in range(12):
                        ps = PF()
                        proj_chunk(ps[:, 0:128], 1792 + c * 128)
                        pr = praw[c % 2]
                        cp(pr[:, 3:131], ps[:, 0:128], eng="act")
                        cp(pr[:, 0:3], halo_g[:, c, :], eng="dve")
                        cp(halo_g[:, c, :], pr[:, 128:131], eng="dve")
                        acc = tf[0]
                        o_, _w = PV_OFF["cw"]
                        ts(acc[:, :], pr[:, 0:128], pv[:, o_ + c * 4:o_ + c * 4 + 1], ALU.mult)
                        for tap in range(1, 4):
                            stt(acc[:, :], pr[:, tap:tap + 128], pv[:, o_ + c * 4 + tap:o_ + c * 4 + tap + 1], acc[:, :],
                                ALU.mult, ALU.add)
                        sl_ = tf[1]
                        act(sl_[:, :], acc[:, :], AF.Silu)
                        h = c % 4
                        if c < 8:
                            sq = tf[2]
                            act(sq[:, :], sl_[:, :], AF.Square)
                            psn = PF()
                            mm(psn[:, 0:128], C("ones"), sq[:, :])
                            rn = tf[3]
                            act(rn[:, :], psn[:, 0:128], AF.Sqrt, bias=1e-6)
                            recip(rn[:, :], rn[:, :])
                            if c < 4:
                                stt(qT.sub(h), sl_[:, :], 128.0 ** -0.5, rn[:, :], ALU.mult, ALU.mult)
                            else:
                                tt(kT.sub(h), sl_[:, :], rn[:, :], ALU.mult)
                        else:
                            cp(vTb.sub(h), sl_[:, :], eng="dve")
                    for h in range(4):
                        ps = PF()
                        proj_chunk(ps[:, 0:128], 3328 + h * 128)
                        act(siluz.sub(h), ps[:, 0:128], AF.Silu)

                    chk("gconv")
                    for pair in range(2):
                        stA = []
                        for hi in range(2):
                            h = pair * 2 + hi

                            def g_(o, h=h):
                                return gt[:, o + h:o + h + 1]
                            diag, Ds, DTi = tf[4], tf[5], tf[6]
                            t0b = tf[9 + hi]
                            if hi == 0:
                                attnT, kbd, kdec, vbt = tb[0], tb[2], tb[3], tb[4]
                            else:
                                attnT, kbd, kdec, vbt = (AM[0][:, 0:128], AM[0][:, 128:256], AM[1][:, 0:128], AM[1][:, 128:256])
                            V_ = (lambda t: t[:, :]) if hi == 0 else (lambda t: t)
                            bA, bB = NTP[2 * hi], NTP[2 * hi + 1]
                            ts(diag[:, :], C("ident"), g_(8), ALU.mult)
                            psR = PF()
                            mm(psR[:, 0:128], C("ones"), diag[:, :], start=True, stop=False)
                            mm(psR[:, 0:128], C("ident"), C("mblo"), start=False, stop=True)
                            mm(psR[:, 128:256], C("ones"), diag[:, :], start=True, stop=False)
                            mm(psR[:, 128:256], C("ident"), C("mbup"), start=False, stop=True)
                            act(Ds[:, :], psR[:, 0:128], AF.Exp, scale=-1.0, bias=g_(8))
                            act(DTi[:, :], psR[:, 128:256], AF.Exp, bias=g_(32))
                            psK = PF()
                            mm(psK[:, 0:128], kT.sub(h), kT.sub(h))
                            mm(psK[:, 128:256], kT.sub(h), qT.sub(h))
                            stt(bA[:, 0:128], psK[:, 0:128], g_(36), Ds[:, :], ALU.mult, ALU.mult)
                            tt(V_(attnT), psK[:, 128:256], DTi[:, :], ALU.mult)
                            psT = PF()
                            mm(psT[:, 0:128], bA[:, 0:128], C("ident"))
                            pb = PB()
                            tr(pb[:, 128:256], kT.sub(h), identb[:, :])
                            tr(pb[:, 256:384], vTb.sub(h), identb[:, :])
                            cp(t0b[:, :], psT[:, 0:128], eng="act")
                            act(V_(kbd), pb[:, 128:256], AF.Copy, scale=g_(28))
                            act(V_(kdec), pb[:, 128:256], AF.Copy, scale=g_(20))
                            act(V_(vbt), pb[:, 256:384], AF.Copy, scale=g_(4))
                            stA.append((h, g_, V_(attnT), V_(kbd), V_(kdec), V_(vbt), bA, bB, t0b))
                        neumann_batch([(bA[:, 0:128], t0b[:, :], tinvT[hi][:, :], bA, bB)
                                       for hi, (h, g_, attnT, kbd, kdec, vbt, bA, bB, t0b) in enumerate(stA)])
                        for hi, (h, g_, attnT, kbd, kdec, vbt, bA, bB, t0b) in enumerate(stA):
                            o1s, o32 = tf[7], tf[8]
                            nwk, vn = tb[5], Xb
                            tiv = tinvT[hi]
                            psW = PF()
                            mm(psW[:, 0:128], kbd, tiv[:, :])
                            ts(nwk[:, :], psW[:, 0:128], -1.0, ALU.mult)
                            psV = PF()
                            mm(psV[:, 0:128], tiv[:, :], vbt, start=True, stop=False)
                            mm(psV[:, 0:128], nwk[:, :], Sb.sub(h), start=False, stop=True)
                            cp(vn[:, :], psV[:, 0:128], eng="act")
                            psO = PF()
                            mm(psO[:, 0:128], qT.sub(h), Sb.sub(h))
                            mm(psO[:, 128:256], attnT, vn[:, :])
                            act(o1s[:, :], psO[:, 0:128], AF.Copy, scale=g_(16))
                            tt(o32[:, :], psO[:, 128:256], o1s[:, :], ALU.add)
                            psS = PF()
                            mm(psS[:, 0:128], kdec, vn[:, :])
                            stt(S32.sub(h), S32.sub(h), g_(24), psS[:, 0:128], ALU.mult, ALU.add)
                            cp(Sb.sub(h), S32.sub(h), eng="act")
                            memset(sc[:, 4:5], 0.0)
                            act(o1s[:, :], o32[:, :], AF.Square, accum=sc[:, 4:5])
                            act(sc[:, 5:6], sc[:, 4:5], AF.Sqrt, bias=1e-6, scale=1.0 / 128)
                            recip(sc[:, 6:7], sc[:, 5:6])
                            onb = tb[1]
                            ts(onb[:, :], o32[:, :], sc[:, 6:7], ALU.mult)
                            pb2 = PB()
                            tr(pb2[:, 0:128], onb[:, :], identb[:, :])
                            act(o1s[:, :], pb2[:, 0:128], AF.Copy, scale=pvv("nw"))
                            tt(ybT.sub(h), o1s[:, :], siluz.sub(h), ALU.mult)

                    chk("gdn")
                    for oc in range(8):
                        osl = slice(oc * 128, (oc + 1) * 128)
                        ps = PF()
                        for jj in range(4):
                            mm(ps[:, 0:128], rprojb[:, jj, osl], yfin.sub(jj), start=(jj == 0), stop=(jj == 3))
                        for jj in range(4):
                            mm(ps[:, 128:256], gprojb[:, jj, osl], ybT.sub(jj), start=(jj == 0), stop=(jj == 3))
                        proj_chunk(ps[:, 256:384], 3848 + oc * 128)
                        proj_chunk(ps[:, 384:512], 4872 + oc * 128)
                        sga, sgb = tf[0], tf[1]
                        act(sga[:, :], ps[:, 256:384], AF.Sigmoid)
                        act(sgb[:, :], ps[:, 384:512], AF.Sigmoid)
                        tt(sga[:, :], sga[:, :], ps[:, 0:128], ALU.mult)
                        tt(sgb[:, :], sgb[:, :], ps[:, 128:256], ALU.mult)
                        tt(mixT.sub(oc), sga[:, :], sgb[:, :], ALU.add)

                    for nb in range(2):
                        ps = PF()
                        for kc in range(8):
                            mm(ps[:, 0:512], mixT.sub(kc), woutb[:, kc, nb * 512:(nb + 1) * 512], start=(kc == 0), stop=(kc == 7))
                        tt(x1[:, nb * 512:(nb + 1) * 512], ps[:, 0:512], xt[:, nb * 512:(nb + 1) * 512], ALU.add)
                    S.dma("sp", x1_d[it * 128:(it + 1) * 128, :], x1[:, :], sembuf=x1.bufs[0])

                S.barrier()
            chk("phase1")
            S.stack = gstack

            with ExitStack() as st2:
                S.stack = st2
                upb = Tile(S, "upb", [128, 8, 2 * FH], BF16)
                downb = Tile(S, "downb", [128, 22, D], BF16)
                fgb = Tile(S, "fgb", [128, D], F32)
                with ExitStack() as stg_stack:
                    S.stack = stg_stack
                    stg = [Tile(S, f"stgb{i}", [128, 1536], F32) for i in range(2)]
                    k = [0]

                    def load_cast2(dst, src, ncols, scale=None):
                        s = stg[k[0] % 2]
                        S.dma("sp", s[:, 0:ncols], src)
                        if scale is not None:
                            if k[0] % 2 == 0:
                                act(dst, s[:, 0:ncols], AF.Copy, scale=scale)
                            else:
                                ts(dst, s[:, 0:ncols], scale, ALU.mult)
                        else:
                            cp(dst, s[:, 0:ncols], eng="act" if k[0] % 2 == 0 else "dve")
                        k[0] += 1

                    CB2 = 1408
                    for kc in range(8):
                        for cb in range(4):
                            load_cast2(upb[:, kc, cb * CB2:(cb + 1) * CB2], up_d[kc * 128:(kc + 1) * 128, cb * CB2:(cb + 1) * CB2],
                                       CB2, scale=pvv("g2n", kc))
                    for kc in range(22):
                        load_cast2(downb[:, kc, :], down_d[kc * 128:(kc + 1) * 128, :], D)
                    S.dma("sp", fgb[:, :], View(fg_d.ap.partition_broadcast(128), [fg_d.buf]))
                    S.barrier()
                S.stack = st2

                chk("setup2")
                x1t = Tile(S, "x1t", [128, D], F32)
                sc2 = Tile(S, "sc2", [128, 16], F32)
                junk2 = Tile(S, "junk2", [128, D], BF16)
                u2 = Tile(S, "u2", [128, D], BF16)
                u2T = Tile(S, "u2T", [128, 8, 128], BF16)
                hbuf = Tile(S, "hbuf", [128, 44, 130], F32, split=True)
                hc = [Tile(S, f"hc{i}", [128, 2, 128], F32) for i in range(2)]
                actT = Tile(S, "actT", [128, 22, 128], BF16, split=True)
                x2 = Tile(S, "x2", [128, D], F32)
                ot = Tile(S, "ot", [128, D], F32)
                memset(hbuf[:, :, :], 0.0)
                fo, _ = PV_OFF["fcw"]

                for it in range(NT):
                    S.dma("sp", x1t[:, :], x1_d[it * 128:(it + 1) * 128, :])
                    memset(sc2[:, 0:1], 0.0)
                    act(junk2[:, :], x1t[:, :], AF.Square, accum=sc2[:, 0:1])
                    act(sc2[:, 1:2], sc2[:, 0:1], AF.Sqrt, bias=1e-6, scale=1.0 / D)
                    recip(sc2[:, 2:3], sc2[:, 1:2])
                    ts(u2[:, :], x1t[:, :], sc2[:, 2:3], ALU.mult)
                    pb = PB()
                    for kc in range(8):
                        tr(pb[:, kc * 128:(kc + 1) * 128], u2[:, kc * 128:(kc + 1) * 128], identb[:, :])
                    S.op("act", lambda e: e.activation(out=u2T.t[:, :, :], in_=pb.t[:, 0:1024].rearrange("p (a b) -> p a b", a=8), func=AF.Copy),
                         [u2T[:, :, :]], [pb[:, :]])
                    chk("norm2")
                    for c in range(22):
                        ps = PF()
                        for half, cc in enumerate((c, c + 22)):
                            for kc in range(8):
                                mm(ps[:, half * 128:(half + 1) * 128], upb[:, kc, cc * 128:(cc + 1) * 128], u2T[:, kc, :],
                                   start=(kc == 0), stop=(kc == 7))
                        hct = hc[c % 2]
                        for half, cc in enumerate((c, c + 22)):
                            def hv(lo, hi_, cc=cc):
                                return hbuf.sub(cc, (slice(None), cc, slice(lo, hi_)))
                            cp(hv(2, 130), ps[:, half * 128:(half + 1) * 128], eng="act")
                            act(hct[:, half, :], hv(0, 128), AF.Copy, scale=pv[:, fo + cc * 3:fo + cc * 3 + 1])
                            for tap in range(1, 3):
                                stt(hct[:, half, :], hv(tap, tap + 128), pv[:, fo + cc * 3 + tap:fo + cc * 3 + tap + 1],
                                    hct[:, half, :], ALU.mult, ALU.add)
                        act(hct[:, 0, :], hct[:, 0, :], AF.Silu)
                        tt(actT.sub(c), hct[:, 0, :], hct[:, 1, :], ALU.mult)
                    cp(hbuf[:, :, 0:2], hbuf[:, :, 128:130], eng="dve")
                    chk("up")
                    for nb in range(2):
                        ps = PF()
                        for c in range(22):
                            mm(ps[:, 0:512], actT.sub(c), downb[:, c, nb * 512:(nb + 1) * 512], start=(c == 0), stop=(c == 21))
                        tt(x2[:, nb * 512:(nb + 1) * 512], ps[:, 0:512], x1t[:, nb * 512:(nb + 1) * 512], ALU.add)
                    chk("down")
                    memset(sc2[:, 4:5], 0.0)
                    act(junk2[:, :], x2[:, :], AF.Square, accum=sc2[:, 4:5])
                    act(sc2[:, 5:6], sc2[:, 4:5], AF.Sqrt, bias=1e-6, scale=1.0 / D)
                    recip(sc2[:, 6:7], sc2[:, 5:6])
                    stt(ot[:, :], x2[:, :], sc2[:, 6:7], fgb[:, :], ALU.mult, ALU.mult)
                    chk("fin")
                    S.dma("sp", y_d[it * 128:(it + 1) * 128, :], ot[:, :], sembuf=ot.bufs[0])
                S.barrier()
            S.stack = gstack
        except StopBuild:
            pass
        S.stack = gstack
        S.dead = False
        S.barrier()
        build.ninst = S.ninst
    return nc


def make_in_maps(inp, T, ncores):
    f = lambda a: np.ascontiguousarray(np.asarray(a, np.float32))
    pvh = make_pv(inp)
    consts = make_consts()
    shared = {
        "w_in": f(inp["w_in"][0]), "w2": f(inp["rwkv_w2"][0]), "a2": f(inp["rwkv_a2"][0]), "g2": f(inp["rwkv_g2"][0]),
        "rproj": f(inp["rwkv_proj"][0]), "gproj": f(inp["gdn_proj"][0]), "wout": f(inp["w_out"][0]),
        "ffn_up": f(inp["ffn_up"][0]), "ffn_down": f(inp["ffn_down"][0]), "final_g": f(inp["final_g"]),
        "pv": pvh, "consts": consts,
    }
    x = np.asarray(inp["x"], np.float32)
    maps = []
    for c in range(ncores):
        m = dict(shared)
        m["x"] = np.ascontiguousarray(x[c, :T])
        maps.append(m)
    return maps


_NC_CACHE = {}


def kernel(**inputs):
    x = np.asarray(inputs["x"])
    B, T, _ = x.shape
    if T not in _NC_CACHE:
        _NC_CACHE[T] = build(T)
    nc = _NC_CACHE[T]
    maps = make_in_maps(inputs, T, B)
    res = run_bass_kernel_spmd(nc, maps, core_ids=list(range(B)))
    out = np.stack([np.asarray(r["y"], np.float32) for r in res.results], axis=0)
    return out
```

```python
from contextlib import ExitStack
import numpy as np
import concourse.bass as bass
import concourse.mybir as mybir
from concourse.bass_utils import run_bass_kernel_spmd

F32 = mybir.dt.float32
BF16 = mybir.dt.bfloat16
AF = mybir.ActivationFunctionType
ALU = mybir.AluOpType

D = 1024
INW = 5896
FH = 2816
BIG = 32768.0
SEM_LIMIT = 30000


class Buf:
    __slots__ = ("name", "w", "r", "dsem", "dcnt")

    def __init__(self, name):
        self.name = name
        self.w = None
        self.r = {}
        self.dsem = None
        self.dcnt = 0


class View:
    __slots__ = ("ap", "bufs")

    def __init__(self, ap, bufs):
        self.ap = ap
        self.bufs = bufs


class Tile:
    def __init__(self, sched, name, shape, dtype, space="sbuf", split=False):
        nc = sched.nc
        if space == "sbuf":
            self.t = sched.stack.enter_context(nc.sbuf_tensor("t_" + name, list(shape), dtype))
        else:
            self.t = sched.stack.enter_context(nc.psum_tensor("t_" + name, list(shape), dtype))
        self.name = name
        self.split = split
        if split:
            self.bufs = [Buf(f"{name}.{i}") for i in range(shape[1])]
        else:
            self.bufs = [Buf(name)]

    def __getitem__(self, key):
        return View(self.t[key], self.bufs)

    def sub(self, i, key=None):
        ap = self.t[:, i] if key is None else self.t[key]
        return View(ap, [self.bufs[i]] if self.split else self.bufs)


class DramT:
    def __init__(self, ap, name):
        self.ap = ap
        self.buf = Buf(name)

    def __getitem__(self, key):
        return View(self.ap[key], [self.buf])


class EngState:
    def __init__(self, name, eng):
        self.name = name
        self.eng = eng
        self.sem = None
        self.cnt = 0
        self.pending = False
        self.seen = {}


class Sched:
    def __init__(self, nc, stack):
        self.nc = nc
        self.stack = stack
        self.semstack = stack
        self.nsem = 0
        self.E = {}
        for n, e in (("pe", nc.tensor), ("act", nc.scalar), ("dve", nc.vector), ("pool", nc.gpsimd), ("sp", nc.sync)):
            es = EngState(n, e)
            es.sem = self.new_sem(n)
            self.E[n] = es
        self.all_dsems = []
        self.ninst = 0
        self.dead = False

    def new_sem(self, name):
        self.nsem += 1
        return self.semstack.enter_context(self.nc.semaphore(f"{name}_{self.nsem}"))

    def _need(self, reads, writes):
        need = {}

        def add(ev):
            if ev is None:
                return
            s, v = ev
            k = id(s)
            if k not in need or need[k][1] < v:
                need[k] = (s, v)

        for b in reads:
            add(b.w)
        for b in writes:
            add(b.w)
            for k, (s, v) in b.r.items():
                add((s, v))
        return need

    def _wait(self, es, need, skip_self=True):
        for k, (s, v) in need.items():
            if skip_self and s is es.sem:
                continue
            if es.seen.get(k, 0) >= v:
                continue
            es.eng.wait_ge(s, v)
            es.seen[k] = v

    @staticmethod
    def _bufs(views):
        out = []
        for v in views:
            if isinstance(v, View):
                out.extend(v.bufs)
        return out

    def op(self, en, fn, outs, ins, inc=True, order_self=False):
        if self.dead:
            return
        es = self.E[en]
        R = self._bufs(ins)
        W = self._bufs(outs)
        need = self._need(R, W)
        self._wait(es, need, skip_self=(en == "pe"))
        ins_ = fn(es.eng)
        self.ninst += 1
        if es.cnt >= SEM_LIMIT and inc and not es.pending:
            es.sem = self.new_sem(es.name)
            es.cnt = 0
        if inc:
            es.cnt += 1
            ins_.then_inc(es.sem, 1)
            ev = (es.sem, es.cnt)
            es.pending = False
        else:
            ev = (es.sem, es.cnt + 1)
            es.pending = True
        k = id(ev[0])
        for b in R:
            if k not in b.r or b.r[k][1] < ev[1]:
                b.r[k] = ev
        for b in W:
            b.w = ev
            b.r = {}

    def dma(self, qn, out, in_, sembuf=None):
        if self.dead:
            return
        es = self.E[qn]
        R = self._bufs([in_])
        W = self._bufs([out])
        need = self._need(R, W)
        self._wait(es, need, skip_self=False)
        sb = sembuf if sembuf is not None else (W[0] if W else R[0])
        if sb.dsem is None:
            sb.dsem = self.new_sem("d_" + sb.name.replace(".", "_"))
            self.all_dsems.append(sb)
        es.eng.dma_start(out=out.ap, in_=in_.ap).then_inc(sb.dsem, 16)
        self.ninst += 1
        sb.dcnt += 16
        ev = (sb.dsem, sb.dcnt)
        k = id(ev[0])
        for b in R:
            if k not in b.r or b.r[k][1] < ev[1]:
                b.r[k] = ev
        for b in W:
            b.w = ev
            b.r = {}

    def barrier(self):
        if self.dead:
            return
        for es in self.E.values():
            for fs in self.E.values():
                if fs.cnt == 0 or (fs is es and es.name == "pe"):
                    continue
                if es.seen.get(id(fs.sem), 0) < fs.cnt:
                    es.eng.wait_ge(fs.sem, fs.cnt)
                    es.seen[id(fs.sem)] = fs.cnt
            for sb in self.all_dsems:
                if es.seen.get(id(sb.dsem), 0) < sb.dcnt:
                    es.eng.wait_ge(sb.dsem, sb.dcnt)
                    es.seen[id(sb.dsem)] = sb.dcnt


def A(v):
    return v.ap if isinstance(v, View) else v


PV_FIELDS = [("g1", 8), ("mu", 14), ("w0", 4), ("a0", 4), ("kk", 4), ("ka", 4), ("rk", 4), ("lnw", 4),
             ("lnb", 4), ("cw", 48), ("nw", 1), ("g2n", 8), ("fcw", 132), ("alog", 4), ("dtb", 4)]
PV_OFF = {}
_o = 0
for _n, _w in PV_FIELDS:
    PV_OFF[_n] = (_o, _w)
    _o += _w
PV_W = _o

CONST_NAMES = ["ident", "ones", "su", "iu", "su2", "iu2", "sl", "mblo", "mbup", "blk", "blkm"]
CI = {n: i for i, n in enumerate(CONST_NAMES)}
NCONST = len(CONST_NAMES)


def make_consts():
    i = np.arange(128)
    s = i[:, None]
    t = i[None, :]
    ident = (s == t).astype(np.float32)
    ones = np.ones((128, 128), np.float32)
    su = (s < t).astype(np.float32)
    iu = (s <= t).astype(np.float32)
    sl = (t < s).astype(np.float32)
    mblo = np.where(t >= s, BIG, 0.0).astype(np.float32)
    mbup = np.where(t < s, -BIG, 0.0).astype(np.float32)
    blk = ((s // 64) == (t // 64)).astype(np.float32)
    blkm = blk / 64.0
    return np.concatenate([ident, ones, su, iu, su, iu, sl, mblo, mbup, blk, blkm], axis=1)


def chunked(v, n):
    return np.ascontiguousarray(np.asarray(v, np.float32).reshape(n, 128).T)


def make_pv(inp):
    pv = np.zeros((128, PV_W), np.float32)

    def put(name, arr):
        o, w = PV_OFF[name]
        assert arr.shape == (128, w), (name, arr.shape)
        pv[:, o:o + w] = arr

    put("g1", chunked(inp["norm1_g"][0], 8))
    put("mu", chunked(inp["rwkv_mu"][0], 14))
    put("w0", chunked(inp["rwkv_w0"][0], 4))
    put("a0", chunked(inp["rwkv_a0"][0], 4))
    put("kk", chunked(inp["rwkv_k_k"][0], 4))
    put("ka", chunked(inp["rwkv_k_a"][0], 4))
    put("rk", chunked(inp["rwkv_r_k"][0].reshape(-1), 4))
    put("lnw", chunked(inp["rwkv_ln_w"][0], 4))
    put("lnb", chunked(inp["rwkv_ln_b"][0], 4))
    cw = np.asarray(inp["gdn_conv_w"][0], np.float32)
    put("cw", np.ascontiguousarray(cw.reshape(4, 12, 128).transpose(2, 1, 0).reshape(128, 48)))
    put("nw", np.asarray(inp["gdn_norm_w"][0], np.float32).reshape(128, 1))
    put("g2n", chunked(inp["norm2_g"][0], 8))
    fcw = np.asarray(inp["ffn_conv_w"][0], np.float32)
    put("fcw", np.ascontiguousarray(fcw.reshape(3, 44, 128).transpose(2, 1, 0).reshape(128, 132)))
    put("alog", np.broadcast_to(np.asarray(inp["gdn_a_log"][0], np.float32)[None, :], (128, 4)))
    put("dtb", np.broadcast_to(np.asarray(inp["gdn_dt_bias"][0], np.float32)[None, :], (128, 4)))
    return pv


class StopBuild(Exception):
    pass


def build(T, stop=None):
    NT = T // 128

    def chk(name):
        if stop == name:
            chk.S.dead = True
    nc = bass.Bass("TRN2", target_bir_lowering=False)

    def dram_in(name, shape):
        return DramT(nc.dram_tensor(name, list(shape), F32, kind="ExternalInput").ap(), name)

    x_d = dram_in("x", [T, D])
    win_d = dram_in("w_in", [D, INW])
    w2_d = dram_in("w2", [64, 512])
    a2_d = dram_in("a2", [64, 512])
    g2_d = dram_in("g2", [128, 512])
    rproj_d = dram_in("rproj", [512, D])
    gproj_d = dram_in("gproj", [512, D])
    wout_d = dram_in("wout", [D, D])
    up_d = dram_in("ffn_up", [D, 2 * FH])
    down_d = dram_in("ffn_down", [FH, D])
    fg_d = dram_in("final_g", [D])
    pv_d = dram_in("pv", [128, PV_W])
    c_d = dram_in("consts", [128, NCONST * 128])
    y_d = DramT(nc.dram_tensor("y", [T, D], F32, kind="ExternalOutput").ap(), "y")
    x1_d = DramT(nc.dram_tensor("x1s", [T, D], F32, kind="Internal").ap(), "x1s")

    with ExitStack() as gstack:
        S = Sched(nc, gstack)
        chk.S = S

        def PE(fn, outs, ins, inc=True):
            S.op("pe", fn, outs, ins, inc=inc)

        def mm(out, lhsT, rhs, start=True, stop=True):
            S.op("pe", lambda e: e.matmul(A(out), lhsT=A(lhsT), rhs=A(rhs), start=start, stop=stop),
                 [out], [lhsT, rhs], inc=stop)

        def tr(out, in_, ident):
            S.op("pe", lambda e: e.transpose(A(out), A(in_), A(ident)), [out], [in_, ident])

        def act(out, in_, func, bias=None, scale=None, accum=None):
            kw = {}
            ins = [in_]
            if bias is not None:
                kw["bias"] = A(bias)
                ins.append(bias)
            if scale is not None:
                kw["scale"] = A(scale)
                ins.append(scale)
            outs = [out]
            if accum is not None:
                kw["accum_out"] = A(accum)
                outs.append(accum)
            S.op("act", lambda e: e.activation(out=A(out), in_=A(in_), func=func, **kw), outs, ins)

        def tt(out, a, b, op, eng="dve"):
            S.op(eng, lambda e: e.tensor_tensor(out=A(out), in0=A(a), in1=A(b), op=op), [out], [a, b],
                 order_self=(eng == "pool"))

        def ts(out, a, s1, op0, s2=None, op1=None, eng="dve"):
            if op1 is None:
                S.op(eng, lambda e: e.tensor_scalar(out=A(out), in0=A(a), scalar1=A(s1), scalar2=None, op0=op0),
                     [out], [a, s1], order_self=(eng == "pool"))
            else:
                S.op(eng, lambda e: e.tensor_scalar(out=A(out), in0=A(a), scalar1=A(s1), scalar2=A(s2), op0=op0, op1=op1),
                     [out], [a, s1, s2], order_self=(eng == "pool"))

        def stt(out, a, s, b, op0, op1, eng="dve"):
            S.op(eng, lambda e: e.scalar_tensor_tensor(out=A(out), in0=A(a), scalar=A(s), in1=A(b), op0=op0, op1=op1),
                 [out], [a, s, b], order_self=(eng == "pool"))

        def cp(out, in_, eng="dve"):
            if eng == "act":
                act(out, in_, AF.Copy)
            else:
                S.op(eng, lambda e: e.tensor_copy(out=A(out), in_=A(in_)), [out], [in_], order_self=(eng == "pool"))

        def memset(out, val, eng="dve"):
            S.op(eng, lambda e: e.memset(A(out), val), [out], [], order_self=(eng == "pool"))

        def recip(out, in_):
            S.op("dve", lambda e: e.reciprocal(out=A(out), in_=A(in_)), [out], [in_])

        def scan_cumsum(out, ones, data):
            S.op("dve", lambda e: e.tensor_tensor_scan(out=A(out), data0=A(ones), data1=A(data), initial=0.0,
                                                       op0=ALU.mult, op1=ALU.add), [out], [ones, data])

        c32 = Tile(S, "c32", [128, NCONST * 128], F32)
        pv = Tile(S, "pv", [128, PV_W], F32)
        identb = Tile(S, "identb", [128, 128], BF16)
        dv = Tile(S, "dv", [128, 16], F32)
        psf = [Tile(S, f"psf{i}", [128, 512], F32, space="psum") for i in range(6)]
        psb = [Tile(S, f"psb{i}", [128, 1024], BF16, space="psum") for i in range(2)]
        rr = {"f": 0, "b": 0}

        def PF():
            t = psf[rr["f"] % 6]
            rr["f"] += 1
            return t

        def PB():
            t = psb[rr["b"] % 2]
            rr["b"] += 1
            return t

        def C(name, w=1):
            i = CI[name]
            return c32[:, i * 128:(i + w) * 128]

        def pvv(name, j=None, w=1):
            o, wd = PV_OFF[name]
            if j is None:
                return pv[:, o:o + wd]
            return pv[:, o + j:o + j + w]

        S.dma("sp", c32[:, :], c_d[:, :])
        S.dma("sp", pv[:, :], pv_d[:, :])

        try:
            cp(identb[:, :], C("ident"))
            ts(dv[:, 0:4], pvv("ka"), -1.0, ALU.mult, 1.0, ALU.add)
            act(dv[:, 4:8], pvv("alog"), AF.Exp)
            ts(dv[:, 4:8], dv[:, 4:8], -1.0, ALU.mult)

            with ExitStack() as st1:
                S.stack = st1
                winb = Tile(S, "winb", [128, 8, INW], BF16)
                rprojb = Tile(S, "rprojb", [128, 4, D], BF16)
                gprojb = Tile(S, "gprojb", [128, 4, D], BF16)
                woutb = Tile(S, "woutb", [128, 8, D], BF16)
                lorab = Tile(S, "lorab", [128, 512], BF16)
                g2b = Tile(S, "g2b", [128, 512], BF16)

                with ExitStack() as stg_stack:
                    S.stack = stg_stack
                    stg = [Tile(S, f"stg{i}", [128, 1536], F32) for i in range(2)]
                    k = [0]

                    def load_cast(dst, src, ncols, scale=None, prow=slice(0, 128)):
                        s = stg[k[0] % 2]
                        S.dma("sp", s[prow, 0:ncols], src)
                        if scale is not None:
                            if k[0] % 2 == 0:
                                act(dst, s[prow, 0:ncols], AF.Copy, scale=scale)
                            else:
                                ts(dst, s[prow, 0:ncols], scale, ALU.mult)
                        else:
                            cp(dst, s[prow, 0:ncols], eng="act" if k[0] % 2 == 0 else "dve")
                        k[0] += 1

                    CB = 1474
                    for kc in range(8):
                        for cb in range(4):
                            load_cast(winb[:, kc, cb * CB:(cb + 1) * CB], win_d[kc * 128:(kc + 1) * 128, cb * CB:(cb + 1) * CB],
                                      CB, scale=pvv("g1", kc))
                    for kc in range(4):
                        load_cast(rprojb[:, kc, :], rproj_d[kc * 128:(kc + 1) * 128, :], D)
                        load_cast(gprojb[:, kc, :], gproj_d[kc * 128:(kc + 1) * 128, :], D)
                    for kc in range(8):
                        load_cast(woutb[:, kc, :], wout_d[kc * 128:(kc + 1) * 128, :], D)
                    load_cast(lorab[0:64, :], w2_d[:, :], 512, prow=slice(0, 64))
                    load_cast(lorab[64:128, :], a2_d[:, :], 512, prow=slice(64, 128))
                    load_cast(g2b[:, :], g2_d[:, :], 512)
                    S.barrier()
                S.stack = st1

                chk("setup1")
                xts = [Tile(S, f"xt{i}", [128, D], F32) for i in range(2)]
                sc = Tile(S, "sc", [128, 16], F32)
                junk = Tile(S, "junk", [128, D], BF16)
                ub = Tile(S, "ub", [128, D], BF16)
                uT = Tile(S, "uT", [128, 8, 128], BF16)
                halo_r = Tile(S, "halo_r", [128, 14], F32)
                praw = [Tile(S, f"praw{i}", [128, 132], F32) for i in range(2)]
                pl = Tile(S, "pl", [128, 14, 128], F32, split=True)
                tw_al = Tile(S, "tw_al", [128, 128], BF16)
                sgl = Tile(S, "sgl", [128, 128], BF16)
                gT = Tile(S, "gT", [128, 4, 128], F32, split=True)
                bon = Tile(S, "bon", [128, 4, 128], F32, split=True)
                NF = 14
                tf = [Tile(S, f"tf{i}", [128, 128], F32) for i in range(NF)]
                ar = Tile(S, "ar", [128, 4, 256], BF16, split=True)
                btl = Tile(S, "btl", [128, 4, 128], BF16, split=True)
                ktl = Tile(S, "ktl", [128, 4, 128], BF16, split=True)
                tb = [Tile(S, f"tb{i}", [128, 128], BF16) for i in range(6)]
                tokm = Tile(S, "tokm", [128, 4, 512], BF16, split=True)
                vpad = [Tile(S, f"vpad{h}", [128, 4, 128], BF16, split=True) for h in range(2)]
                upad = [Tile(S, f"upad{h}", [128, 128], BF16) for h in range(2)]
                AM = [Tile(S, f"AM{h}", [128, 512], BF16) for h in range(2)]
                NTP = [Tile(S, f"NTP{i}", [128, 384], F32) for i in range(4)]
                tinvT = [Tile(S, f"tinvT{h}", [128, 128], BF16) for h in range(2)]
                Xb = Tile(S, "Xb", [128, 128], BF16)
                yfin = Tile(S, "yfin", [128, 4, 128], BF16, split=True)
                H32 = Tile(S, "H32", [128, 4, 128], F32, split=True)
                Hbd = Tile(S, "Hbd", [128, 4, 128], BF16, split=True)
                S32 = Tile(S, "S32", [128, 4, 128], F32, split=True)
                Sb = Tile(S, "Sb", [128, 4, 128], BF16, split=True)
                halo_g = Tile(S, "halo_g", [128, 12, 3], F32)
                qT = Tile(S, "qT", [128, 4, 128], BF16, split=True)
                kT = Tile(S, "kT", [128, 4, 128], BF16, split=True)
                vTb = Tile(S, "vTb", [128, 4, 128], BF16, split=True)
                siluz = Tile(S, "siluz", [128, 4, 128], F32, split=True)
                gt = Tile(S, "gt", [128, 64], F32)
                ybT = Tile(S, "ybT", [128, 4, 128], BF16, split=True)
                mixT = Tile(S, "mixT", [128, 8, 128], BF16, split=True)
                x1 = Tile(S, "x1", [128, D], F32)

                memset(halo_r[:, :], 0.0)
                memset(halo_g[:, :, :], 0.0)
                memset(H32[:, :, :], 0.0)
                memset(Hbd[:, :, :], 0.0)
                memset(S32[:, :, :], 0.0)
                memset(Sb[:, :, :], 0.0)
                for h in range(2):
                    memset(vpad[h][:, :, :], 0.0)
                    memset(upad[h][:, :], 0.0)

                def proj_chunk(ps_view, col0, ncols=128):
                    for kc in range(8):
                        mm(ps_view, winb[:, kc, col0:col0 + ncols], uT[:, kc, :], start=(kc == 0), stop=(kc == 7))

                def neumann_batch(systems):
                    idb = C("ident")
                    cur = [None] * len(systems)
                    for lvl in range(7):
                        for si, (N0, T0, outT, bA, bB) in enumerate(systems):
                            ps = PF()
                            if lvl == 0:
                                Nk, Tk, Pk = N0, T0, None
                            else:
                                c_ = cur[si]
                                Nk, Tk, Pk = c_[:, 0:128], c_[:, 128:256], c_[:, 256:384]
                            if lvl <= 5:
                                mm(ps[:, 0:128], Tk, Nk)
                            if lvl <= 4:
                                mm(ps[:, 128:256], Nk, Tk)
                            if lvl == 0:
                                mm(ps[:, 256:384], idb, T0, start=True, stop=False)
                                mm(ps[:, 256:384], idb, idb, start=False, stop=True)
                            else:
                                mm(ps[:, 256:384], Nk, Pk, start=True, stop=False)
                                mm(ps[:, 256:384], idb, Pk, start=False, stop=True)
                            e1 = "act" if si % 2 == 0 else "dve"
                            e2 = "dve" if si % 2 == 0 else "act"
                            if lvl == 6:
                                cp(outT, ps[:, 256:384], eng=e1)
                            else:
                                nxt = bB if lvl % 2 == 0 else bA
                                if lvl == 5:
                                    cp(nxt[:, 0:128], ps[:, 0:128], eng=e1)
                                    cp(nxt[:, 256:384], ps[:, 256:384], eng=e2)
                                else:
                                    cp(nxt[:, 0:384], ps[:, 0:384], eng=e1)
                                cur[si] = nxt

                for it in range(NT):
                    if it == 0:
                        S.dma("sp", xts[0][:, :], x_d[0:128, :])
                    xt = xts[it % 2]
                    if it + 1 < NT:
                        S.dma("sp", xts[(it + 1) % 2][:, :], x_d[(it + 1) * 128:(it + 2) * 128, :])
                    memset(sc[:, 0:1], 0.0)
                    act(junk[:, :], xt[:, :], AF.Square, accum=sc[:, 0:1])
                    act(sc[:, 1:2], sc[:, 0:1], AF.Sqrt, bias=1e-6, scale=1.0 / D)
                    recip(sc[:, 2:3], sc[:, 1:2])
                    ts(ub[:, :], xt[:, :], sc[:, 2:3], ALU.mult)
                    pb = PB()
                    for kc in range(8):
                        tr(pb[:, kc * 128:(kc + 1) * 128], ub[:, kc * 128:(kc + 1) * 128], identb[:, :])
                    S.op("act", lambda e: e.activation(out=uT.t[:, :, :], in_=pb.t[:, 0:1024].rearrange("p (a b) -> p a b", a=8), func=AF.Copy),
                         [uT[:, :, :]], [pb[:, :]])

                    chk("norm")
                    for c in [12, 13] + list(range(12)):
                        ps = PF()
                        proj_chunk(ps[:, 0:128], c * 128)
                        pr = praw[c % 2]
                        cp(pr[:, 1:129], ps[:, 0:128], eng="act")
                        cp(pr[:, 0:1], halo_r[:, c:c + 1], eng="dve")
                        cp(halo_r[:, c:c + 1], pr[:, 128:129], eng="dve")
                        d_ = tf[0]
                        tt(d_[:, :], pr[:, 0:128], pr[:, 1:129], ALU.subtract)
                        stt(pl.sub(c), d_[:, :], pvv("mu", c), pr[:, 1:129], ALU.mult, ALU.add)
                        if c == 12:
                            act(tw_al[0:64, :], pl.sub(12, (slice(0, 64), 12)), AF.Tanh)
                            cp(tw_al[64:128, :], pl.sub(12, (slice(64, 128), 12)), eng="dve")
                        if c == 13:
                            act(sgl[:, :], pl.sub(13), AF.Sigmoid)

                    chk("rproj")
                    for j in range(4):
                        r_ = pl.sub(j)
                        k_ = pl.sub(4 + j)
                        v_ = pl.sub(8 + j)
                        cs = slice(j * 128, (j + 1) * 128)
                        ps = PF()
                        chk("q0")
                        mm(ps[:, 0:128], lorab[0:64, cs], tw_al[0:64, :])
                        chk("q1")
                        psL2 = PF()
                        mm(psL2[:, 0:128], lorab[64:128, cs], tw_al[64:128, :])
                        chk("q2")
                        mm(ps[:, 256:384], g2b[:, cs], sgl[:, :])
                        chk("q3")
                        lw, asig, cw_, cwp, ew, einv, ehat, eprev, kq, kkn, kpr, bvec, t0_, t1_ = tf
                        act(lw[:, :], ps[:, 0:128], AF.Sigmoid, bias=pvv("w0", j))
                        chk("q4")
                        ts(lw[:, :], lw[:, :], -0.6065306597126334, ALU.mult)
                        chk("q5")
                        act(asig[:, :], psL2[:, 0:128], AF.Sigmoid, bias=pvv("a0", j))
                        chk("q6")
                        cp(gT.sub(j), ps[:, 256:384], eng="act")
                        chk("r1")
                        scan_cumsum(cw_[:, :], C("ones"), lw[:, :])
                        tt(cwp[:, :], cw_[:, :], lw[:, :], ALU.subtract)
                        act(ew[:, :], cw_[:, :], AF.Exp)
                        act(einv[:, :], cw_[:, :], AF.Exp, scale=-1.0)
                        act(ehat[:, :], cw_[:, :], AF.Exp, scale=-1.0, bias=cw_[:, 127:128])
                        act(eprev[:, :], cwp[:, :], AF.Exp)
                        chk("r2")
                        ts(kq[:, :], k_, pvv("kk", j), ALU.mult)
                        act(t0_[:, :], kq[:, :], AF.Square)
                        ps2 = PF()
                        mm(ps2[:, 0:128], C("blk"), t0_[:, :])
                        act(t1_[:, :], ps2[:, 0:128], AF.Sqrt, bias=1e-6)
                        recip(t1_[:, :], t1_[:, :])
                        tt(kkn[:, :], kq[:, :], t1_[:, :], ALU.mult)
                        chk("r3")
                        ts(t0_[:, :], asig[:, :], pvv("ka", j), ALU.mult, dv[:, j:j + 1], ALU.add)
                        tt(kpr[:, :], k_, t0_[:, :], ALU.mult)
                        tt(bvec[:, :], kkn[:, :], asig[:, :], ALU.mult)
                        stt(ar.sub(j, (slice(None), j, slice(0, 128))), kkn[:, :], -1.0, eprev[:, :], ALU.mult, ALU.mult)
                        tt(ar.sub(j, (slice(None), j, slice(128, 256))), r_, ew[:, :], ALU.mult)
                        tt(btl.sub(j), bvec[:, :], einv[:, :], ALU.mult)
                        tt(ktl.sub(j), kpr[:, :], einv[:, :], ALU.mult)
                        bh, kh, vb_ = tb[0], tb[1], tb[2]
                        tt(bh[:, :], bvec[:, :], ehat[:, :], ALU.mult)
                        tt(kh[:, :], kpr[:, :], ehat[:, :], ALU.mult)
                        cp(vb_[:, :], v_, eng="act")
                        stt(t0_[:, :], r_, pvv("rk", j), kpr[:, :], ALU.mult, ALU.mult)
                        mm(ps2[:, 128:256], C("blk"), t0_[:, :])
                        tt(bon.sub(j), ps2[:, 128:256], v_, ALU.mult)
                        chk("r4")
                        pb = PB()
                        tr(pb[:, 0:128], vb_[:, :], identb[:, :])
                        chk("t1")
                        tr(pb[:, 128:256], ar.sub(j, (slice(None), j, slice(0, 128))), identb[:, :])
                        chk("t2")
                        tr(pb[:, 256:384], bh[:, :], identb[:, :])
                        tr(pb[:, 384:512], kh[:, :], identb[:, :])
                        chk("t4")
                        cp(tokm.sub(j), pb[:, 0:512], eng="act")
                        chk("t5")
                        cp(vpad[0].sub(j, (slice(None), j, slice(0, 64))), tokm.sub(j, (slice(None), j, slice(0, 64))), eng="dve")
                        chk("t6")
                        cp(vpad[1].sub(j, (slice(None), j, slice(64, 128))), tokm.sub(j, (slice(None), j, slice(64, 128))), eng="dve")
                        chk("r5")
                        sysl = []
                        for hh in range(2):
                            prow = slice(hh * 64, hh * 64 + 64)
                            psA = PF()
                            mm(psA[:, 0:256], btl.sub(j, (prow, j)), ar.sub(j, (prow, j)))
                            mm(psA[:, 256:512], ktl.sub(j, (prow, j)), ar.sub(j, (prow, j)))
                            psB = PF()
                            mm(psB[:, 0:128], ar.sub(j, (prow, j, slice(0, 128))), btl.sub(j, (prow, j)))
                            tt(AM[hh][:, :], psA[:, 0:512], C("su", 4), ALU.mult)
                            bA, bB = NTP[2 * hh], NTP[2 * hh + 1]
                            tt(bA[:, 0:128], psB[:, 0:128], C("sl"), ALU.mult)
                            t0f = cwp if hh == 0 else einv
                            tt(t0f[:, :], psA[:, 0:128], C("su"), ALU.mult)
                            sysl.append((bA[:, 0:128], t0f[:, :], tinvT[hh][:, :], bA, bB))
                        neumann_batch(sysl)
                        chk("r6")
                        psX = PF()
                        mm(psX[:, 0:128], ar.sub(j, (slice(None), j, slice(0, 128))), Hbd.sub(j), start=True, stop=False)
                        mm(psX[:, 0:128], AM[0][:, 256:384], vpad[0].sub(j), start=False, stop=False)
                        mm(psX[:, 0:128], AM[1][:, 256:384], vpad[1].sub(j), start=False, stop=True)
                        cp(Xb[:, :], psX[:, 0:128], eng="act")
                        psU = PF()
                        mm(psU[:, 0:64], tinvT[0][:, :], Xb[:, 0:64])
                        mm(psU[:, 64:128], tinvT[1][:, :], Xb[:, 64:128])
                        cp(upad[0][:, 0:64], psU[:, 0:64], eng="act")
                        cp(upad[1][:, 64:128], psU[:, 64:128], eng="dve")
                        chk("r7")
                        psY = PF()
                        mm(psY[:, 0:128], Hbd.sub(j), ar.sub(j, (slice(None), j, slice(128, 256))), start=True, stop=False)
                        for hh in range(2):
                            mm(psY[:, 0:128], upad[hh][:, :], AM[hh][:, 128:256], start=False, stop=False)
                            mm(psY[:, 0:128], vpad[hh].sub(j), AM[hh][:, 384:512], start=False, stop=(hh == 1))
                        psH = PF()
                        utile = tb[3]
                        tt(utile[:, :], upad[0][:, :], upad[1][:, :], ALU.add)
                        mm(psH[:, 0:128], tokm.sub(j, (slice(None), j, slice(256, 384))), utile[:, :], start=True, stop=False)
                        mm(psH[:, 0:128], tokm.sub(j, (slice(None), j, slice(384, 512))), tokm.sub(j, (slice(None), j, slice(0, 128))),
                           start=False, stop=True)
                        tt(t0_[:, :], psH[:, 0:128], C("blk"), ALU.mult)
                        stt(H32.sub(j), H32.sub(j), ew[:, 127:128], t0_[:, :], ALU.mult, ALU.add)
                        cp(Hbd.sub(j), H32.sub(j), eng="act")
                        chk("r8")
                        y32, ysq = t1_, kq
                        cp(y32[:, :], psY[:, 0:128], eng="act")
                        act(ysq[:, :], y32[:, :], AF.Square)
                        psG = PF()
                        mm(psG[:, 0:128], C("blkm"), y32[:, :])
                        mm(psG[:, 128:256], C("blkm"), ysq[:, :])
                        msq, var = kkn, kpr
                        act(msq[:, :], psG[:, 0:128], AF.Square)
                        tt(var[:, :], psG[:, 128:256], msq[:, :], ALU.subtract)
                        act(var[:, :], var[:, :], AF.Sqrt, bias=64e-5)
                        recip(var[:, :], var[:, :])
                        tt(y32[:, :], y32[:, :], psG[:, 0:128], ALU.subtract)
                        tt(y32[:, :], y32[:, :], var[:, :], ALU.mult)
                        ts(y32[:, :], y32[:, :], pvv("lnw", j), ALU.mult, pvv("lnb", j), ALU.add)
                        tt(y32[:, :], y32[:, :], bon.sub(j), ALU.add)
                        tt(yfin.sub(j), y32[:, :], gT.sub(j), ALU.mult)

                    chk("rwkv")
                    psg = PF()
                    for kc in range(8):
                        mm(psg[:, 0:8], uT[:, kc, :], winb[:, kc, 3840:3848], start=(kc == 0), stop=(kc == 7))
                    tt(gt[:, 0:4], psg[:, 0:4], pvv("dtb"), ALU.add)
                    act(gt[:, 4:8], psg[:, 4:8], AF.Sigmoid)
                    act(gt[:, 0:4], gt[:, 0:4], AF.Exp)
                    act(gt[:, 0:4], gt[:, 0:4], AF.Ln, bias=1.0)
                    tt(gt[:, 0:4], gt[:, 0:4], dv[:, 4:8], ALU.mult)
                    psg2 = PF()
                    mm(psg2[:, 0:4], C("iu"), gt[:, 0:4])
                    mm(psg2[:, 4:8], C("ones"), gt[:, 0:4])
                    cp(gt[:, 8:16], psg2[:, 0:8], eng="act")
                    act(gt[:, 16:20], gt[:, 8:12], AF.Exp)
                    tt(gt[:, 20:24], gt[:, 12:16], gt[:, 8:12], ALU.subtract)
                    act(gt[:, 20:24], gt[:, 20:24], AF.Exp)
                    act(gt[:, 24:28], gt[:, 12:16], AF.Exp)
                    tt(gt[:, 28:32], gt[:, 4:8], gt[:, 16:20], ALU.mult)
                    ts(gt[:, 32:36], gt[:, 8:12], -1.0, ALU.mult)
                    ts(gt[:, 36:40], gt[:, 4:8], -1.0, ALU.mult)

                    chk("gates")
                    for c in range(12):
                        ps = PF()
                        proj_chunk(ps[:, 0:128], 1792 + c * 128)
                        pr = praw[c % 2]
                        cp(pr[:, 3:131], ps[:, 0:128], eng="act")
                        cp(pr[:, 0:3], halo_g[:, c, :], eng="dve")
                        cp(halo_g[:, c, :], pr[:, 128:131], eng="dve")
                        acc = tf[0]
                        o_, _w = PV_OFF["cw"]
                        ts(acc[:, :], pr[:, 0:128], pv[:, o_ + c * 4:o_ + c * 4 + 1], ALU.mult)
                        for tap in range(1, 4):
                            stt(acc[:, :], pr[:, tap:tap + 128], pv[:, o_ + c * 4 + tap:o_ + c * 4 + tap + 1], acc[:, :],
                                ALU.mult, ALU.add)
                        sl_ = tf[1]
                        act(sl_[:, :], acc[:, :], AF.Silu)
                        h = c % 4
                        if c < 8:
                            sq = tf[2]
                            act(sq[:, :], sl_[:, :], AF.Square)
                            psn = PF()
                            mm(psn[:, 0:128], C("ones"), sq[:, :])
                            rn = tf[3]
                            act(rn[:, :], psn[:, 0:128], AF.Sqrt, bias=1e-6)
                            recip(rn[:, :], rn[:, :])
                            if c < 4:
                                stt(qT.sub(h), sl_[:, :], 128.0 ** -0.5, rn[:, :], ALU.mult, ALU.mult)
                            else:
                                tt(kT.sub(h), sl_[:, :], rn[:, :], ALU.mult)
                        else:
                            cp(vTb.sub(h), sl_[:, :], eng="dve")
                    for h in range(4):
                        ps = PF()
                        proj_chunk(ps[:, 0:128], 3328 + h * 128)
                        act(siluz.sub(h), ps[:, 0:128], AF.Silu)

                    chk("gconv")
                    for pair in range(2):
                        stA = []
                        for hi in range(2):
                            h = pair * 2 + hi

                            def g_(o, h=h):
                                return gt[:, o + h:o + h + 1]
                            diag, Ds, DTi = tf[4], tf[5], tf[6]
                            t0b = tf[9 + hi]
                            if hi == 0:
                                attnT, kbd, kdec, vbt = tb[0], tb[2], tb[3], tb[4]
                            else:
                                attnT, kbd, kdec, vbt = (AM[0][:, 0:128], AM[0][:, 128:256], AM[1][:, 0:128], AM[1][:, 128:256])
                            V_ = (lambda t: t[:, :]) if hi == 0 else (lambda t: t)
                            bA, bB = NTP[2 * hi], NTP[2 * hi + 1]
                            ts(diag[:, :], C("ident"), g_(8), ALU.mult)
                            psR = PF()
                            mm(psR[:, 0:128], C("ones"), diag[:, :], start=True, stop=False)
                            mm(psR[:, 0:128], C("ident"), C("mblo"), start=False, stop=True)
                            mm(psR[:, 128:256], C("ones"), diag[:, :], start=True, stop=False)
                            mm(psR[:, 128:256], C("ident"), C("mbup"), start=False, stop=True)
                            act(Ds[:, :], psR[:, 0:128], AF.Exp, scale=-1.0, bias=g_(8))
                            act(DTi[:, :], psR[:, 128:256], AF.Exp, bias=g_(32))
                            psK = PF()
                            mm(psK[:, 0:128], kT.sub(h), kT.sub(h))
                            mm(psK[:, 128:256], kT.sub(h), qT.sub(h))
                            stt(bA[:, 0:128], psK[:, 0:128], g_(36), Ds[:, :], ALU.mult, ALU.mult)
                            tt(V_(attnT), psK[:, 128:256], DTi[:, :], ALU.mult)
                            psT = PF()
                            mm(psT[:, 0:128], bA[:, 0:128], C("ident"))
                            pb = PB()
                            tr(pb[:, 128:256], kT.sub(h), identb[:, :])
                            tr(pb[:, 256:384], vTb.sub(h), identb[:, :])
                            cp(t0b[:, :], psT[:, 0:128], eng="act")
                            act(V_(kbd), pb[:, 128:256], AF.Copy, scale=g_(28))
                            act(V_(kdec), pb[:, 128:256], AF.Copy, scale=g_(20))
                            act(V_(vbt), pb[:, 256:384], AF.Copy, scale=g_(4))
                            stA.append((h, g_, V_(attnT), V_(kbd), V_(kdec), V_(vbt), bA, bB, t0b))
                        neumann_batch([(bA[:, 0:128], t0b[:, :], tinvT[hi][:, :], bA, bB)
                                       for hi, (h, g_, attnT, kbd, kdec, vbt, bA, bB, t0b) in enumerate(stA)])
                        for hi, (h, g_, attnT, kbd, kdec, vbt, bA, bB, t0b) in enumerate(stA):
                            o1s, o32 = tf[7], tf[8]
                            nwk, vn = tb[5], Xb
                            tiv = tinvT[hi]
                            psW = PF()
                            mm(psW[:, 0:128], kbd, tiv[:, :])
                            ts(nwk[:, :], psW[:, 0:128], -1.0, ALU.mult)
                            psV = PF()
                            mm(psV[:, 0:128], tiv[:, :], vbt, start=True, stop=False)
                            mm(psV[:, 0:128], nwk[:, :], Sb.sub(h), start=False, stop=True)
                            cp(vn[:, :], psV[:, 0:128], eng="act")
                            psO = PF()
                            mm(psO[:, 0:128], qT.sub(h), Sb.sub(h))
                            mm(psO[:, 128:256], attnT, vn[:, :])
                            act(o1s[:, :], psO[:, 0:128], AF.Copy, scale=g_(16))
                            tt(o32[:, :], psO[:, 128:256], o1s[:, :], ALU.add)
                            psS = PF()
                            mm(psS[:, 0:128], kdec, vn[:, :])
                            stt(S32.sub(h), S32.sub(h), g_(24), psS[:, 0:128], ALU.mult, ALU.add)
                            cp(Sb.sub(h), S32.sub(h), eng="act")
                            memset(sc[:, 4:5], 0.0)
                            act(o1s[:, :], o32[:, :], AF.Square, accum=sc[:, 4:5])
                            act(sc[:, 5:6], sc[:, 4:5], AF.Sqrt, bias=1e-6, scale=1.0 / 128)
                            recip(sc[:, 6:7], sc[:, 5:6])
                            onb = tb[1]
                            ts(onb[:, :], o32[:, :], sc[:, 6:7], ALU.mult)
                            pb2 = PB()
                            tr(pb2[:, 0:128], onb[:, :], identb[:, :])
                            act(o1s[:, :], pb2[:, 0:128], AF.Copy, scale=pvv("nw"))
                            tt(ybT.sub(h), o1s[:, :], siluz.sub(h), ALU.mult)

                    chk("gdn")
                    for oc in range(8):
                        osl = slice(oc * 128, (oc + 1) * 128)
                        ps = PF()
                        for jj in range(4):
                            mm(ps[:, 0:128], rprojb[:, jj, osl], yfin.sub(jj), start=(jj == 0), stop=(jj == 3))
                        for jj in range(4):
                            mm(ps[:, 128:256], gprojb[:, jj, osl], ybT.sub(jj), start=(jj == 0), stop=(jj == 3))
                        proj_chunk(ps[:, 256:384], 3848 + oc * 128)
                        proj_chunk(ps[:, 384:512], 4872 + oc * 128)
                        sga, sgb = tf[0], tf[1]
                        act(sga[:, :], ps[:, 256:384], AF.Sigmoid)
                        act(sgb[:, :], ps[:, 384:512], AF.Sigmoid)
                        tt(sga[:, :], sga[:, :], ps[:, 0:128], ALU.mult)
                        tt(sgb[:, :], sgb[:, :], ps[:, 128:256], ALU.mult)
                        tt(mixT.sub(oc), sga[:, :], sgb[:, :], ALU.add)

                    for nb in range(2):
                        ps = PF()
                        for kc in range(8):
                            mm(ps[:, 0:512], mixT.sub(kc), woutb[:, kc, nb * 512:(nb + 1) * 512], start=(kc == 0), stop=(kc == 7))
                        tt(x1[:, nb * 512:(nb + 1) * 512], ps[:, 0:512], xt[:, nb * 512:(nb + 1) * 512], ALU.add)
                    S.dma("sp", x1_d[it * 128:(it + 1) * 128, :], x1[:, :], sembuf=x1.bufs[0])

                S.barrier()
            chk("phase1")
            S.stack = gstack

            with ExitStack() as st2:
                S.stack = st2
                upb = Tile(S, "upb", [128, 8, 2 * FH], BF16)
                downb = Tile(S, "downb", [128, 22, D], BF16)
                fgb = Tile(S, "fgb", [128, D], F32)
                with ExitStack() as stg_stack:
                    S.stack = stg_stack
                    stg = [Tile(S, f"stgb{i}", [128, 1536], F32) for i in range(2)]
                    k = [0]

                    def load_cast2(dst, src, ncols, scale=None):
                        s = stg[k[0] % 2]
                        S.dma("sp", s[:, 0:ncols], src)
                        if scale is not None:
                            if k[0] % 2 == 0:
                                act(dst, s[:, 0:ncols], AF.Copy, scale=scale)
                            else:
                                ts(dst, s[:, 0:ncols], scale, ALU.mult)
                        else:
                            cp(dst, s[:, 0:ncols], eng="act" if k[0] % 2 == 0 else "dve")
                        k[0] += 1

                    CB2 = 1408
                    for kc in range(8):
                        for cb in range(4):
                            load_cast2(upb[:, kc, cb * CB2:(cb + 1) * CB2], up_d[kc * 128:(kc + 1) * 128, cb * CB2:(cb + 1) * CB2],
                                       CB2, scale=pvv("g2n", kc))
                    for kc in range(22):
                        load_cast2(downb[:, kc, :], down_d[kc * 128:(kc + 1) * 128, :], D)
                    S.dma("sp", fgb[:, :], View(fg_d.ap.partition_broadcast(128), [fg_d.buf]))
                    S.barrier()
                S.stack = st2

                chk("setup2")
                x1ts = [Tile(S, f"x1t{i}", [128, D], F32) for i in range(2)]
                sc2 = Tile(S, "sc2", [128, 16], F32)
                junk2 = Tile(S, "junk2", [128, D], BF16)
                u2 = Tile(S, "u2", [128, D], BF16)
                u2T = Tile(S, "u2T", [128, 8, 128], BF16)
                hbuf = Tile(S, "hbuf", [128, 44, 130], F32, split=True)
                hc = [Tile(S, f"hc{i}", [128, 2, 128], F32) for i in range(2)]
                actT = Tile(S, "actT", [128, 22, 128], BF16, split=True)
                x2 = Tile(S, "x2", [128, D], F32)
                ot = Tile(S, "ot", [128, D], F32)
                memset(hbuf[:, :, :], 0.0)
                fo, _ = PV_OFF["fcw"]

                for it in range(NT):
                    if it == 0:
                        S.dma("sp", x1ts[0][:, :], x1_d[0:128, :])
                    x1t = x1ts[it % 2]
                    if it + 1 < NT:
                        S.dma("sp", x1ts[(it + 1) % 2][:, :], x1_d[(it + 1) * 128:(it + 2) * 128, :])
                    memset(sc2[:, 0:1], 0.0)
                    act(junk2[:, :], x1t[:, :], AF.Square, accum=sc2[:, 0:1])
                    act(sc2[:, 1:2], sc2[:, 0:1], AF.Sqrt, bias=1e-6, scale=1.0 / D)
                    recip(sc2[:, 2:3], sc2[:, 1:2])
                    ts(u2[:, :], x1t[:, :], sc2[:, 2:3], ALU.mult)
                    pb = PB()
                    for kc in range(8):
                        tr(pb[:, kc * 128:(kc + 1) * 128], u2[:, kc * 128:(kc + 1) * 128], identb[:, :])
                    S.op("act", lambda e: e.activation(out=u2T.t[:, :, :], in_=pb.t[:, 0:1024].rearrange("p (a b) -> p a b", a=8), func=AF.Copy),
                         [u2T[:, :, :]], [pb[:, :]])
                    chk("norm2")
                    for c in range(22):
                        ps = PF()
                        for half, cc in enumerate((c, c + 22)):
                            for kc in range(8):
                                mm(ps[:, half * 128:(half + 1) * 128], upb[:, kc, cc * 128:(cc + 1) * 128], u2T[:, kc, :],
                                   start=(kc == 0), stop=(kc == 7))
                        hct = hc[c % 2]
                        for half, cc in enumerate((c, c + 22)):
                            def hv(lo, hi_, cc=cc):
                                return hbuf.sub(cc, (slice(None), cc, slice(lo, hi_)))
                            cp(hv(2, 130), ps[:, half * 128:(half + 1) * 128], eng="act")
                            act(hct[:, half, :], hv(0, 128), AF.Copy, scale=pv[:, fo + cc * 3:fo + cc * 3 + 1])
                            for tap in range(1, 3):
                                stt(hct[:, half, :], hv(tap, tap + 128), pv[:, fo + cc * 3 + tap:fo + cc * 3 + tap + 1],
                                    hct[:, half, :], ALU.mult, ALU.add)
                        act(hct[:, 0, :], hct[:, 0, :], AF.Silu)
                        tt(actT.sub(c), hct[:, 0, :], hct[:, 1, :], ALU.mult)
                    cp(hbuf[:, :, 0:2], hbuf[:, :, 128:130], eng="dve")
                    chk("up")
                    for nb in range(2):
                        ps = PF()
                        for c in range(22):
                            mm(ps[:, 0:512], actT.sub(c), downb[:, c, nb * 512:(nb + 1) * 512], start=(c == 0), stop=(c == 21))
                        tt(x2[:, nb * 512:(nb + 1) * 512], ps[:, 0:512], x1t[:, nb * 512:(nb + 1) * 512], ALU.add)
                    chk("down")
                    memset(sc2[:, 4:5], 0.0)
                    act(junk2[:, :], x2[:, :], AF.Square, accum=sc2[:, 4:5])
                    act(sc2[:, 5:6], sc2[:, 4:5], AF.Sqrt, bias=1e-6, scale=1.0 / D)
                    recip(sc2[:, 6:7], sc2[:, 5:6])
                    stt(ot[:, :], x2[:, :], sc2[:, 6:7], fgb[:, :], ALU.mult, ALU.mult)
                    chk("fin")
                    S.dma("sp", y_d[it * 128:(it + 1) * 128, :], ot[:, :], sembuf=ot.bufs[0])
                S.barrier()
            S.stack = gstack
        except StopBuild:
            pass
        S.stack = gstack
        S.dead = False
        S.barrier()
        build.ninst = S.ninst
    return nc


def make_in_maps(inp, T, ncores):
    f = lambda a: np.ascontiguousarray(np.asarray(a, np.float32))
    pvh = make_pv(inp)
    consts = make_consts()
    shared = {
        "w_in": f(inp["w_in"][0]), "w2": f(inp["rwkv_w2"][0]), "a2": f(inp["rwkv_a2"][0]), "g2": f(inp["rwkv_g2"][0]),
        "rproj": f(inp["rwkv_proj"][0]), "gproj": f(inp["gdn_proj"][0]), "wout": f(inp["w_out"][0]),
        "ffn_up": f(inp["ffn_up"][0]), "ffn_down": f(inp["ffn_down"][0]), "final_g": f(inp["final_g"]),
        "pv": pvh, "consts": consts,
    }
    x = np.asarray(inp["x"], np.float32)
    maps = []
    for c in range(ncores):
        m = dict(shared)
        m["x"] = np.ascontiguousarray(x[c, :T])
        maps.append(m)
    return maps


_NC_CACHE = {}


def kernel(**inputs):
    x = np.asarray(inputs["x"])
    B, T, _ = x.shape
    if T not in _NC_CACHE:
        _NC_CACHE[T] = build(T)
    nc = _NC_CACHE[T]
    maps = make_in_maps(inputs, T, B)
    res = run_bass_kernel_spmd(nc, maps, core_ids=list(range(B)))
    out = np.stack([np.asarray(r["y"], np.float32) for r in res.results], axis=0)
    return out
```
